# Optimizing a Trainium2 kernel written in Bass

```python
import jax, jax.numpy as jnp
from jax import lax
import numpy as np

D_MODEL = 2048
BATCH = 2
SEQ = 4096
DEPTH = 1

CHUNK = 64
HEAD_DIM = 128
N_HEADS_SB = 8
N_HEADS_FOX = 8
W_SB = N_HEADS_SB * HEAD_DIM
W_FOX = N_HEADS_FOX * HEAD_DIM
MIX_WIDTH = W_SB + W_FOX
IN_COLS = 3 * W_SB + 3 * W_FOX + N_HEADS_FOX
D_FF = 4 * D_MODEL
Q_BLOCK = 128
EPS = 1e-6

kernel_name = "hybrid_stickbreaking_fox_block"


def rmsnorm(x, g):
    xf = x.astype(jnp.float32)
    var = jnp.mean(xf * xf, axis=-1, keepdims=True)
    return (xf * lax.rsqrt(var + EPS) * g.astype(jnp.float32)).astype(x.dtype)


def head_rmsnorm(o, g):
    of = o.astype(jnp.float32)
    of = of * lax.rsqrt(jnp.mean(of * of, axis=-1, keepdims=True) + EPS)
    b, h, s, d = o.shape
    of = jnp.transpose(of, (0, 2, 1, 3)).reshape(b, s, h * d)
    return (of * g.astype(jnp.float32)).astype(o.dtype)


def split_heads(t, n_heads):
    b, s, _ = t.shape
    return jnp.transpose(t.reshape(b, s, n_heads, HEAD_DIM), (0, 2, 1, 3))


def stick_breaking_attention(q, k, v):
    seq = q.shape[2]
    scale = HEAD_DIM ** -0.5
    outs = []
    for i in range(seq // Q_BLOCK):
        end = (i + 1) * Q_BLOCK
        qb = q[:, :, i * Q_BLOCK:end]
        kb = k[:, :, :end]
        vb = v[:, :, :end]
        z = jnp.einsum('bhqd,bhkd->bhqk', qb, kb).astype(jnp.float32) * scale
        t_pos = i * Q_BLOCK + jnp.arange(Q_BLOCK)
        s_pos = jnp.arange(end)
        causal = s_pos[None, :] < t_pos[:, None]
        log_beta = jax.nn.log_sigmoid(z)
        log_keep = jnp.where(causal, jax.nn.log_sigmoid(-z), 0.0)
        rest = lax.cumsum(log_keep, axis=3, reverse=True) - log_keep
        a = jnp.where(causal, jnp.exp(log_beta + rest), 0.0)
        outs.append(jnp.einsum('bhqk,bhkd->bhqd', a.astype(v.dtype), vb))
    return jnp.concatenate(outs, axis=2)


def forgetting_attention(q, k, v, log_f):
    seq = q.shape[2]
    scale = HEAD_DIM ** -0.5
    c = lax.cumsum(log_f.astype(jnp.float32), axis=2)
    outs = []
    for i in range(seq // Q_BLOCK):
        end = (i + 1) * Q_BLOCK
        qb = q[:, :, i * Q_BLOCK:end]
        kb = k[:, :, :end]
        vb = v[:, :, :end]
        z = jnp.einsum('bhqd,bhkd->bhqk', qb, kb).astype(jnp.float32) * scale
        z = z + c[:, :, i * Q_BLOCK:end, None] - c[:, :, None, :end]
        t_pos = i * Q_BLOCK + jnp.arange(Q_BLOCK)
        s_pos = jnp.arange(end)
        causal = s_pos[None, :] <= t_pos[:, None]
        p = jax.nn.softmax(jnp.where(causal, z, -jnp.inf), axis=-1)
        outs.append(jnp.einsum('bhqk,bhkd->bhqd', p.astype(v.dtype), vb))
    return jnp.concatenate(outs, axis=2)


def setup_inputs(seed: int = 0) -> dict:
    key = jax.random.key(seed)
    ks = jax.random.split(key, 12)
    f32 = jnp.float32
    x = jax.random.normal(ks[0], (BATCH, SEQ, D_MODEL), f32)
    g_attn = 1.0 + 0.05 * jax.random.normal(ks[1], (DEPTH, D_MODEL), f32)
    w_in = jax.random.normal(ks[2], (DEPTH, D_MODEL, IN_COLS), f32) * D_MODEL ** -0.5
    b_f = jax.random.uniform(ks[3], (DEPTH, N_HEADS_FOX), f32, 1.0, 4.0)
    g_out_sb = 1.0 + 0.05 * jax.random.normal(ks[4], (DEPTH, W_SB), f32)
    g_out_fox = 1.0 + 0.05 * jax.random.normal(ks[5], (DEPTH, W_FOX), f32)
    w_out = jax.random.normal(ks[6], (DEPTH, MIX_WIDTH, D_MODEL), f32) * MIX_WIDTH ** -0.5
    g_mlp = 1.0 + 0.05 * jax.random.normal(ks[7], (DEPTH, D_MODEL), f32)
    w_up = jax.random.normal(ks[8], (DEPTH, D_MODEL, D_FF), f32) * D_MODEL ** -0.5
    w_down = jax.random.normal(ks[9], (DEPTH, D_FF, D_MODEL), f32) * D_FF ** -0.5
    g_final = 1.0 + 0.05 * jax.random.normal(ks[10], (D_MODEL,), f32)
    return {"x": x, "g_attn": g_attn, "w_in": w_in, "b_f": b_f,
            "g_out_sb": g_out_sb, "g_out_fox": g_out_fox, "w_out": w_out,
            "g_mlp": g_mlp, "w_up": w_up, "w_down": w_down, "g_final": g_final}


def reference(x, g_attn, w_in, b_f, g_out_sb, g_out_fox, w_out, g_mlp, w_up, w_down, g_final):
    o0 = 3 * W_SB
    o1 = o0 + 3 * W_FOX
    for l in range(DEPTH):
        h = rmsnorm(x, g_attn[l])
        proj = jnp.einsum('bsd,de->bse', h, w_in[l])
        qa = split_heads(proj[..., 0:W_SB], N_HEADS_SB)
        ka = split_heads(proj[..., W_SB:2 * W_SB], N_HEADS_SB)
        va = split_heads(proj[..., 2 * W_SB:o0], N_HEADS_SB)
        qf = split_heads(proj[..., o0:o0 + W_FOX], N_HEADS_FOX)
        kf = split_heads(proj[..., o0 + W_FOX:o0 + 2 * W_FOX], N_HEADS_FOX)
        vf = split_heads(proj[..., o0 + 2 * W_FOX:o1], N_HEADS_FOX)
        f_logit = proj[..., o1:].astype(jnp.float32) + b_f[l].astype(jnp.float32)
        log_f = jnp.transpose(jax.nn.log_sigmoid(f_logit), (0, 2, 1))

        o_sb = stick_breaking_attention(qa, ka, va)
        o_fox = forgetting_attention(qf, kf, vf, log_f)
        mixed = jnp.concatenate([head_rmsnorm(o_sb, g_out_sb[l]),
                                 head_rmsnorm(o_fox, g_out_fox[l])], axis=-1)
        x = x + jnp.einsum('bse,ed->bsd', mixed, w_out[l])

        h2 = rmsnorm(x, g_mlp[l])
        u = jax.nn.relu(jnp.einsum('bsd,df->bsf', h2, w_up[l]))
        x = x + jnp.einsum('bsf,fd->bsd', u * u, w_down[l])
    return rmsnorm(x, g_final)
```

```python
import numpy as np
import concourse.bass as bass
import concourse.mybir as mybir
from concourse.bass_utils import run_bass_kernel_spmd

F32 = mybir.dt.float32
BF16 = mybir.dt.bfloat16
ALU = mybir.AluOpType
AF = mybir.ActivationFunctionType

D = 2048
S = 4096
NCH = 16
TOK = 1024
DFF = 8192
EPS = 1e-6
SCALE = 128 ** -0.5
NEG = -32768.0

ENGS = ["pe", "act", "dve", "pool", "sp"]


class Sched:
    def __init__(self, nc):
        self.nc = nc
        self.ops = {e: [] for e in ENGS}
        self.sem = {}
        self.cnt = {e: 0 for e in ENGS}
        self.waited = {e: {} for e in ENGS}
        self.last = {e: None for e in ENGS}
        self.nsem = 0

    def new_sem(self, name):
        h = self.nc.alloc_semaphore(name)
        self.nsem += 1
        return h

    def init(self):
        for e in ENGS:
            self.sem[e] = self.new_sem("prog_" + e)

    def _waits(self, eng, waits):
        ws = []
        for w in waits:
            if w is None:
                continue
            if isinstance(w, list):
                ws.extend(self._waits(eng, w))
                continue
            sem, val = w
            key = sem.name if hasattr(sem, "name") else id(sem)
            if self.waited[eng].get(key, 0) >= val:
                continue
            self.waited[eng][key] = val
            ws.append((sem, val))
        return ws

    def op(self, eng, fn, waits=(), signal=True):
        ws = self._waits(eng, waits)
        ev = None
        if signal:
            self.cnt[eng] += 1
            ev = (self.sem[eng], self.cnt[eng])
            self.last[eng] = ev
        self.ops[eng].append((fn, ws, ev, 1))
        return ev

    def dma(self, eng, dsem, fn, waits=()):
        ws = self._waits(eng, waits)
        dsem["n"] += 16
        ev = (dsem["h"], dsem["n"])
        self.ops[eng].append((fn, ws, ev, 16))
        return ev

    def dsem(self, name):
        return {"h": self.new_sem(name), "n": 0}

    def all_last(self):
        return [self.last[e] for e in ENGS if self.last[e] is not None]

    def replay(self, eng, e):
        for fn, ws, ev, inc in self.ops[eng]:
            for sem, val in ws:
                e.wait_ge(sem, val)
            ins = fn(e)
            if ev is not None:
                ins.then_inc(ev[0], inc)


def _build(stage=99, dumps=()):
    nc = bass.Bass("TRN2", target_bir_lowering=False)
    sc = Sched(nc)
    sc.init()

    def din(name, shape, dt=F32):
        return nc.dram_tensor(name, list(shape), dt, kind="ExternalInput").ap()

    xT = din("xT", [D, S])
    xown = din("xown", [D, TOK])
    win_sb = din("win_sb", [D, 768])
    win_fx = din("win_fx", [D, 770])
    wout = din("wout", [D, D])
    wup = din("wup", [D, DFF])
    wdn = din("wdn", [DFF, D])
    NCB = 512 + 4096
    NCF = 128 * 3 + 1 + 16 + 2 + 4 + 16 + 16
    cbf_d = din("cbf", [128, NCB])
    cf_d = din("cf", [128, NCF])
    yT = nc.dram_tensor("yT", [D, TOK], F32, kind="ExternalOutput").ap()

    cin = [nc.dram_tensor("cin%d" % h, [128, S], BF16) for h in range(4)]
    cout = [nc.dram_tensor("cout%d" % h, [512, S], BF16) for h in range(4)]
    dscr = [nc.dram_tensor("dscr%d" % h, [96, 128], BF16) for h in range(2)]
    cinA = nc.dram_tensor("cinA", [128, 3072], BF16)
    coutA = nc.dram_tensor("coutA", [512, 3072], BF16)
    cinB = nc.dram_tensor("cinB", [128, 1024], BF16)
    coutB = nc.dram_tensor("coutB", [512, 1024], BF16)

    BASE = 16640
    LIMIT = 229376
    off = {"v": BASE}

    def alloc(name, shape, dt, at=None):
        nb = int(np.prod(shape[1:])) * (4 if dt == F32 else 2)
        nb = (nb + 63) // 64 * 64
        if at is None:
            at = off["v"]
            off["v"] = at + nb
        assert at + nb <= LIMIT, (name, at, nb)
        return nc.alloc_sbuf_tensor_at(name, list(shape), dt, offset=at), at + nb

    cbf, _ = alloc("cbf_sb", [128, NCB], BF16)
    cf, _ = alloc("cf_sb", [128, NCF], F32)
    gs, _ = alloc("gs", [128, 16], F32)
    A1 = off["v"]

    ident_bf = cbf[:, 0:128]
    ones_bf = cbf[:, 128:256]
    trineg = cbf[:, 256:384]
    onesneg = cbf[:, 384:512]

    def nm_sb(j):
        return cbf[:, 512 + 512 * j: 512 + 512 * (j + 1)]

    def nm_fx(j):
        return cbf[:, 2560 + 512 * j: 2560 + 512 * (j + 1)]

    triincl_f = cf[:, 0:128]
    ones_f = cf[:, 128:256]
    ident_f = cf[:, 256:384]
    e0 = cf[:, 384:385]
    gattn = cf[:, 385:401]
    bf_bc = cf[:, 401:403]
    gout = cf[:, 403:407]
    gmlp = cf[:, 407:423]
    gfin = cf[:, 423:439]

    o = A1
    rstd_bc, o = alloc("rstd_bc", [128, S], F32, o)
    rstd_col, o = alloc("rstd_col", [128, 32], F32, o)
    S0 = o
    qTs, kTs, Vts = [None, None], [None, None], [None, None]
    qTs[0], o = alloc("qT0", [128, 2, S], BF16, o)
    kTs[0], o = alloc("kT0", [128, 2, S], BF16, o)
    Vts[0], o = alloc("Vt0", [128, 32, 256], BF16, o)
    F0 = o
    qTs[1], o = alloc("qT1", [128, 2, S], BF16, o)
    kTs[1], o = alloc("kT1", [128, 2, S], BF16, o)
    Vts[1], o = alloc("Vt1", [128, 32, 256], BF16, o)
    xb, o2 = alloc("xb", [128, 2, 16, 512], BF16, F0)
    sq, o2 = alloc("sq", [128, 16, 512], BF16, o2)
    assert o2 <= o
    fz, o = alloc("fz", [128, 64], F32, o)
    spt, o = alloc("spt", [128, 64], F32, o)
    acc, o = alloc("acc", [128, 64], F32, o)
    cneg, o = alloc("cneg", [128, 64], F32, o)
    hib, o = alloc("hib", [128, 64], BF16, o)
    hif, o = alloc("hif", [128, 64], F32, o)
    r1, o = alloc("r1", [128, 64], F32, o)
    midb, o = alloc("midb", [128, 64], BF16, o)
    midf, o = alloc("midf", [128, 64], F32, o)
    r2, o = alloc("r2", [128, 64], F32, o)
    lob, o = alloc("lob", [128, 64], BF16, o)
    Xh, o = alloc("Xh", [128, 2, 96], F32, o)
    trs, o = alloc("trs", [128, 2, 128], BF16, o)
    P1 = o
    wbf, o = alloc("wbf", [128, 16, 770], BF16, P1)
    RT3, _ = alloc("RT3", [128, 2, S], BF16, P1)
    xb2, o = alloc("xb2", [128, 2, 16, 256], BF16, o)
    P2 = o
    NZ = 4
    Eb, o = alloc("Eb", [128, NZ, 512], F32, P2)
    Lb, o = alloc("Lb", [128, NZ, 512], BF16, o)
    Ab, o = alloc("Ab", [128, NZ, 512], BF16, o)
    LS, o = alloc("LS", [128, 2, 512], BF16, o)
    sqo, o = alloc("sqo", [128, 2, 512], BF16, o)
    rr, o = alloc("rr", [128, 2, 512], F32, o)
    rl, o = alloc("rl", [128, 2, 512], F32, o)
    stg, o = alloc("stg", [128, 2, 512], BF16, o)
    wbf0, _ = alloc("wbf0", [128, 16, 770], BF16, P2)
    assert 24640 <= o - P2
    wbs = [wbf0, wbf]

    o = A1
    r2bc, o = alloc("r2bc", [128, TOK], F32, o)
    sq2, o = alloc("sq2", [128, 4, 512], BF16, o)
    wo2a, o = alloc("wo2a", [128, 2, D], BF16, o)
    assert o <= S0
    wu0, o = alloc("wu0", [128, 16, 512], BF16, S0)
    wo, o = alloc("wo", [128, 2, 12, 512], BF16, o)
    wo2b, o = alloc("wo2b", [128, 2, D], BF16, o)
    assert o <= F0
    x1T, o = alloc("x1T", [128, 16, TOK], F32, F0)
    x1n, o = alloc("x1n", [128, 16, TOK], BF16, o)
    PM = o
    mixT, o = alloc("mixT", [128, 16, TOK], BF16, PM)
    wu1, o = alloc("wu1", [128, 16, 512], BF16, S0 + 16384)
    rT, o = alloc("rT", [128, 8, TOK], BF16, o)
    assert o <= F0
    wd, o = alloc("wd", [128, 3, 8, 512], BF16, PM)
    rtmp, o = alloc("rtmp", [128, 2, 512], F32, o)
    wu_t = [wu0, wu1]

    PS = [nc.alloc_psum_tensor("ps%d" % i, [128, 512], F32) for i in range(8)]

    ld_c = sc.dsem("ld_c")
    gcache = {}
    _et = {"sp": mybir.EngineType.SP, "act": mybir.EngineType.Activation}

    def gidx(eng="sp"):
        if eng not in gcache:
            gcache[eng] = nc.partition_id([_et[eng]]) % 4
        return gcache[eng]
    ld_cb = sc.dsem("ld_cb")

    ev_cbf = sc.dma("pool", ld_cb, lambda e: e.dma_start(out=cbf[:, 0:512], in_=cbf_d[:, 0:512]))
    ld_cm = sc.dsem("ld_cm")
    st_m = {}

    def load_masks():
        st_m["ev"] = sc.dma("pool", ld_cm, lambda e: e.dma_start(out=cbf[:, 512:NCB], in_=cbf_d[:, 512:NCB]))
    ev_cf = sc.dma("sp", ld_c, lambda e: e.dma_start(out=cf[:, :], in_=cf_d))
    ev_c = [ev_cf, ev_cbf]
    ev_gs = sc.op("dve", lambda e: e.tensor_scalar(gs[:, :], gattn, SCALE, None, ALU.mult), ev_c)

    xT_v = xT.rearrange("(c p) t -> p c t", p=128)

    x_sem = [[sc.dsem("x%d%d" % (a, b_)) for b_ in range(2)] for a in range(2)]
    w_sem = [sc.dsem("w%d" % q) for q in range(4)]
    w2_sem = [sc.dsem("w2_%d" % q) for q in range(4)]
    w_sems = [w_sem, w2_sem]
    st = {"xb_free": [[], []], "sq_free": [], "bank_free": [None] * 8,
          "phase_bar": [], "wev": {}, "xe": {}, "pe_last_inproj": None}

    def x_tile_dma(j, waits):
        s_ = j % 2
        t0 = 512 * j
        xe = []
        for hf in range(2):
            xe.append(sc.dma("pool", x_sem[s_][hf],
                             lambda e, hf=hf, s_=s_, t0=t0: e.dma_start(
                                 out=xb[:, s_, 8 * hf:8 * hf + 8, :],
                                 in_=xT_v[:, 8 * hf:8 * hf + 8, t0:t0 + 512]),
                             st["xb_free"][s_] + waits))
        return xe

    x2_sem = [sc.dsem("x2_0"), sc.dsem("x2_1")]
    st["x2e"] = {}
    st["x2_free"] = [[], []]

    def x2_tile_dma(t):
        s_ = t % 2
        return sc.dma("pool", x2_sem[s_], lambda e: e.dma_start(
            out=xb2[:, s_, :, :], in_=xT_v[:, :, 256 * t:256 * t + 256]), st["x2_free"][s_])

    def inproj1_items():
        ncol, nv = 770, 258
        banks = [PS[3], PS[6]]
        bfree = [list(st["bar0"]), list(st["bar0"])]
        gc = [0]
        items = []

        def group(kind, t, idx):
            s_ = t % 2
            t0 = 256 * t
            gi = gc[0] % 2
            gc[0] += 1
            bk = banks[gi]
            state = {}

            def chunk(c0):
                wev = st["wev"][1]
                xe = st["x2e"][t]
                for c in range(c0, c0 + 4):
                    if kind == "qk":
                        ev = sc.op("pe", lambda e, c=c: e.matmul(
                            bk[:, 0:256], wbf[:, c, 128 * idx:128 * idx + 128], xb2[:, s_, c, :],
                            start=(c == 0), stop=(c == NCH - 1)),
                            [xe, wev[c][0], wev[c][1]] + bfree[gi], signal=(c == NCH - 1))
                    else:
                        ev = sc.op("pe", lambda e, c=c: e.matmul(
                            bk[:, 0:nv], xb2[:, s_, c, 128 * idx:128 * idx + 128], wbf[:, c, 512:ncol],
                            start=(c == 0), stop=(c == NCH - 1)),
                            [xe, wev[c][1]] + bfree[gi], signal=(c == NCH - 1))
                if c0 + 4 == NCH:
                    if kind == "qk":
                        dst = qTs[1] if idx < 2 else kTs[1]
                        e1 = sc.op("dve", lambda e: e.tensor_tensor(
                            dst[:, idx % 2, t0:t0 + 256], bk[:, 0:256], rstd_bc[:, t0:t0 + 256], ALU.mult), [ev])
                        bfree[gi] = [e1]
                    else:
                        blk = 2 * t + idx
                        e1 = sc.op("dve", lambda e: e.tensor_scalar(
                            Vts[1][:, blk, :], bk[:, 0:256], rstd_col[:, blk:blk + 1], None, ALU.mult), [ev])
                        e2 = sc.op("dve", lambda e: e.scalar_tensor_tensor(
                            fz[:, 2 * blk:2 * blk + 2], bk[:, 256:258], rstd_col[:, blk:blk + 1], bf_bc,
                            ALU.mult, ALU.add), [ev] + ev_c)
                        bfree[gi] = [e1, e2]
                        st["fz_last"] = e2
                        if idx == 1:
                            st["x2_free"][s_] = [ev]
                            st["pe_last_inproj"] = ev
            for c0 in range(0, NCH, 4):
                items.append(lambda c0=c0: chunk(c0))

        for t in range(16):
            if t + 1 < 16 and t + 1 >= 2:
                items.append(lambda t=t: st["x2e"].__setitem__(t + 1, x2_tile_dma(t + 1)))
            for eb in range(4):
                group("qk", t, eb)
            for i in range(2):
                group("v", t, i)
        return items

    def inproj_prep(rnd):
        win = win_sb if rnd == 0 else win_fx
        ncol = 768 if rnd == 0 else 770
        win_v = win.rearrange("(c p) n -> p c n", p=128)
        war = [st["pe_last_inproj"]]
        wb = wbs[rnd]
        wdma = [None] * 4
        st["wev"][rnd] = [None] * NCH
        st["xe"][rnd] = {}
        jobs = []

        def wjob(q):
            wdma[q] = sc.dma("pool", w_sems[rnd][q], lambda e: e.dma_start(
                out=wb[:, 4 * q:4 * q + 4, 0:ncol], in_=win_v[:, 4 * q:4 * q + 4, :]), war)

        def xjob(j):
            if rnd == 0:
                st["xe"][rnd][j] = x_tile_dma(j, [])
            else:
                st["x2e"][j] = x2_tile_dma(j)

        def sjob(c):
            eng = "dve" if (rnd == 1 or c % 2 == 0) else "act"
            w = [wdma[c // 4], ev_gs] + ev_c
            if eng == "dve":
                e1 = sc.op(eng, lambda e: e.tensor_scalar(
                    wb[:, c, 0:256], wb[:, c, 0:256], gs[:, c:c + 1], None, ALU.mult), w)
                e2 = sc.op(eng, lambda e: e.tensor_scalar(
                    wb[:, c, 256:ncol], wb[:, c, 256:ncol], gattn[:, c:c + 1], None, ALU.mult), w)
            else:
                e1 = sc.op(eng, lambda e: e.activation(
                    wb[:, c, 0:256], wb[:, c, 0:256], AF.Copy, scale=gs[:, c:c + 1]), w)
                e2 = sc.op(eng, lambda e: e.activation(
                    wb[:, c, 256:ncol], wb[:, c, 256:ncol], AF.Copy, scale=gattn[:, c:c + 1]), w)
            st["wev"][rnd][c] = [e1, e2]

        if rnd == 0:
            jobs.append(("x", lambda: xjob(0)))
        for q in range(4):
            jobs.append(("w", lambda q=q: wjob(q)))
        if rnd == 0:
            jobs.append(("m", load_masks))
            jobs.append(("x", lambda: xjob(1)))
        else:
            for j in range(2):
                jobs.append(("x", lambda j=j: xjob(j)))
        for c in range(NCH):
            jobs.append(("s", lambda c=c: sjob(c)))
        return jobs

    def inproj(rnd):
        ncol = 768 if rnd == 0 else 770
        nv = ncol - 512
        bar = list(st["phase_bar"])

        def emit_squares(j, xe):
            s_ = j % 2
            se = []
            for hf in range(2):
                se.append(sc.op("act", lambda e, hf=hf, s_=s_: e.activation(
                    sq[:, 8 * hf:8 * hf + 8, :], xb[:, s_, 8 * hf:8 * hf + 8, :], AF.Square),
                    [xe[hf]] + st["sq_free"] + bar))
            return se
        wev = st["wev"][rnd]
        pe_last_tile = None
        for j in range(8):
            s_ = j % 2
            t0 = 512 * j
            if j in st["xe"][rnd]:
                xe = st["xe"][rnd][j]
            else:
                xe = x_tile_dma(j, [])
            readers = []
            if rnd == 0:
                if j == 0:
                    se = emit_squares(0, xe)
                else:
                    se = st["se_next"]
                readers += se
                bk = PS[0]
                for c in range(NCH):
                    ev = sc.op("pe", lambda e, c=c, bk=bk: e.matmul(
                        bk[:, :], ones_bf, sq[:, c, :], start=(c == 0), stop=(c == NCH - 1)),
                        [se[c // 8], st["bank_free"][0]] + ev_c + bar, signal=(c == NCH - 1))
                st["sq_free"] = [ev]
                e1 = sc.op("act", lambda e, bk=bk, t0=t0: e.activation(
                    rstd_bc[:, t0:t0 + 512], bk[:, :], AF.Ln, bias=EPS, scale=1.0 / D), [ev])
                e2 = sc.op("act", lambda e, t0=t0: e.activation(
                    rstd_bc[:, t0:t0 + 512], rstd_bc[:, t0:t0 + 512], AF.Exp, scale=-0.5), [e1])
                st["bank_free"][0] = e1
                st["rstd_ev"][j] = e2
            rs_ev = st["rstd_ev"][j]
            for eb in range(4):
                bi = 2 + (eb % 2)
                bk = PS[bi]
                for c in range(NCH):
                    ev = sc.op("pe", lambda e, c=c, bk=bk, eb=eb, s_=s_: e.matmul(
                        bk[:, :], wbs[rnd][:, c, 128 * eb:128 * eb + 128], xb[:, s_, c, :],
                        start=(c == 0), stop=(c == NCH - 1)),
                        [xe[c // 8], wev[c][0], wev[c][1], st["bank_free"][bi]] + bar, signal=(c == NCH - 1))
                dst = qTs[rnd] if eb < 2 else kTs[rnd]
                e1 = sc.op("dve", lambda e, bk=bk, dst=dst, eb=eb, t0=t0: e.tensor_tensor(
                    dst[:, eb % 2, t0:t0 + 512], bk[:, :], rstd_bc[:, t0:t0 + 512], ALU.mult), [ev, rs_ev] + bar)
                st["bank_free"][bi] = e1
            if rnd == 0:
                for fn_ in st["bg0"].pop(j, []):
                    fn_()
            if rnd == 0 and j + 1 < 8:
                if (j + 1) not in st["xe"][rnd]:
                    st["xe"][rnd][j + 1] = x_tile_dma(j + 1, [])
                st["se_next"] = emit_squares(j + 1, st["xe"][rnd][j + 1])
            if rnd == 0:
                bk1 = PS[1]
                for i in range(4):
                    ev = sc.op("pe", lambda e, i=i, bk1=bk1, t0=t0: e.matmul(
                        bk1[:, i:i + 1], rstd_bc[:, t0 + 128 * i:t0 + 128 * i + 128], e0, start=True, stop=True),
                        [rs_ev, st["bank_free"][1]], signal=(i == 3))
                e3 = sc.op("dve", lambda e, bk1=bk1, j=j: e.tensor_copy(
                    out=rstd_col[:, 4 * j:4 * j + 4], in_=bk1[:, 0:4]), [ev])
                st["bank_free"][1] = e3
                st["rcol_ev"][j] = e3
            rc_ev = st["rcol_ev"][j]
            for i in range(4):
                bi = 4 + (i % 2)
                bk = PS[bi]
                blk = 4 * j + i
                for c in range(NCH):
                    ev = sc.op("pe", lambda e, c=c, bk=bk, i=i, s_=s_: e.matmul(
                        bk[:, 0:nv], xb[:, s_, c, 128 * i:128 * i + 128], wbs[rnd][:, c, 512:ncol],
                        start=(c == 0), stop=(c == NCH - 1)),
                        [xe[c // 8], wev[c][1], st["bank_free"][bi]] + bar, signal=(c == NCH - 1))
                pe_last_tile = ev
                e1 = sc.op("act", lambda e, bk=bk, blk=blk: e.activation(
                    Vts[rnd][:, blk, :], bk[:, 0:256], AF.Copy, scale=rstd_col[:, blk:blk + 1]), [ev, rc_ev] + bar)
                fr = [e1]
                if rnd == 1:
                    e2 = sc.op("dve", lambda e, bk=bk, blk=blk: e.scalar_tensor_tensor(
                        fz[:, 2 * blk:2 * blk + 2], bk[:, 256:258], rstd_col[:, blk:blk + 1], bf_bc, ALU.mult, ALU.add),
                        [ev, rc_ev] + ev_c + bar)
                    fr.append(e2)
                st["bank_free"][bi] = fr
            st["xb_free"][s_] = [pe_last_tile] + readers
        st["pe_last_inproj"] = pe_last_tile
        st["phase_bar"] = sc.all_last()

    st["rstd_ev"] = [None] * 8
    st["rcol_ev"] = [None] * 8

    def fox_c_stages():
        F = {}

        def dep0():
            return [st["fz_last"], st["pe_last_inproj"]]

        def s_a():
            F["e1"] = sc.op("act", lambda e: e.activation(spt[:, :], fz[:, :], AF.Exp, scale=-1.0), dep0())
            F["e2"] = sc.op("act", lambda e: e.activation(spt[:, :], spt[:, :], AF.Ln, bias=1.0), [F["e1"]])
            F["ev"] = sc.op("dve", lambda e: e.memset(acc[:, 0:2], 0.0), dep0())

        def s_adds(b0, b1):
            def f():
                ev = F["ev"]
                for b in range(b0, b1):
                    ev = sc.op("dve", lambda e, b=b: e.tensor_tensor(
                        acc[:, 2 * b + 2:2 * b + 4], acc[:, 2 * b:2 * b + 2], spt[:, 2 * b:2 * b + 2], ALU.add),
                        [ev, F["e2"]])
                F["ev"] = ev
            return f

        def s_mm():
            bk = PS[3]
            sc.op("pe", lambda e: e.matmul(bk[:, 0:64], triincl_f, spt[:, :], start=True, stop=False),
                  [F["e2"]] + ev_c + dep0(), signal=False)
            F["pm"] = sc.op("pe", lambda e: e.matmul(bk[:, 0:64], ones_f, acc[:, :], start=False, stop=True),
                            [F["ev"]])

        def s_split():
            bk = PS[3]
            cn = sc.op("dve", lambda e: e.tensor_copy(out=cneg[:, :], in_=bk[:, 0:64]), [F["pm"]])
            F["cn"] = cn
            cflat = cneg[:, :]
            a = sc.op("dve", lambda e: e.tensor_copy(out=hib[:, :], in_=cflat), [cn])
            a = sc.op("dve", lambda e: e.tensor_copy(out=hif[:, :], in_=hib[:, :]), [a])
            a = sc.op("dve", lambda e: e.tensor_tensor(r1[:, :], cflat, hif[:, :], ALU.subtract), [a])
            a = sc.op("dve", lambda e: e.tensor_copy(out=midb[:, :], in_=r1[:, :]), [a])
            a = sc.op("dve", lambda e: e.tensor_copy(out=midf[:, :], in_=midb[:, :]), [a])
            a = sc.op("dve", lambda e: e.tensor_tensor(r2[:, :], r1[:, :], midf[:, :], ALU.subtract), [a])
            a = sc.op("dve", lambda e: e.tensor_copy(out=lob[:, :], in_=r2[:, :]), [a])
            F["a"] = a

        def s_xh():
            a = F["a"]
            F["zr"] = sc.op("dve", lambda e: e.memset(RT3[:, :, :], 0.0), dep0())
            for h in range(2):
                for k, src in enumerate((hib, midb, lob)):
                    srcv = src[:, :].rearrange("p (b h) -> p b h", h=2)
                    a = sc.op("dve", lambda e, h=h, k=k, srcv=srcv: e.tensor_scalar(
                        Xh[:, h, 32 * k:32 * k + 32], srcv[:, :, h], -1.0, None, ALU.mult), [a])
            F["a"] = a

        def s_tr():
            F["c"] = []
            for h in range(2):
                bkh = PS[6] if h == 0 else PS[3]
                t = sc.op("pe", lambda e, h=h, bkh=bkh: e.matmul(
                    bkh[0:96, 0:128], Xh[:, h, :], ident_f, start=True, stop=True), [F["a"], F["cn"]] + ev_c)
                F["c"].append(sc.op("act", lambda e, h=h, bkh=bkh: e.activation(
                    trs[0:96, h, :], bkh[0:96, 0:128], AF.Copy), [t]))

        def s_dma():
            evs = []
            for h in range(2):
                d1 = sc.dma("sp", ld_c, lambda e, h=h: e.dma_start(out=dscr[h].ap(), in_=trs[0:96, h, :]),
                            [F["c"][h]])
                d2 = sc.dma("sp", ld_c, lambda e, h=h: e.dma_start(
                    out=RT3[0:3, h, :], in_=dscr[h].ap().rearrange("(k b) t -> k (b t)", k=3)), [d1, F["zr"]])
                evs.append(d2)
            st["rt3_ev"] = evs[-1]
            st["cneg_ev"] = F["cn"]
        return [s_a, s_adds(0, 8), s_adds(8, 16), s_adds(16, 24), s_adds(24, 31), s_mm, s_split, s_xh, s_tr, s_dma]

    stg_sem = [sc.dsem("stg0"), sc.dsem("stg1")]
    cc_sem = sc.new_sem("cc")
    st["cc_n"] = 0
    stg_free = [None, None]

    def attention(rnd, bg=(), filler=(), filler_start=0, late=()):
        bar = list(st["phase_bar"])
        nz = 3
        zb = [PS[i] for i in range(nz)] if rnd == 0 else [PS[5], PS[6], PS[7]]
        ob = [PS[4], PS[5]] if rnd == 0 else [PS[0], PS[1]]
        filler = list(filler)
        fpos = [0]
        late = list(late)
        late_pos = [0]
        late_steps = [0, 1, 2, 3, 4, 6, 8, 10, 13, 17]
        lb = [None, None] if rnd == 0 else [PS[2], PS[3]]
        sb_ = PS[7] if rnd == 0 else PS[4]
        tiles = []
        for h in range(2):
            for G in range(8):
                nkb = 4 * G + 4
                for n in range(nkb):
                    tiles.append((h, G, n, nkb - 1 - n, nkb))
        NT = len(tiles)
        z_free = [None] * nz
        e_ln = [None] * NT
        e_pool = [None] * NT
        e_pe1 = [None] * NT
        e_pe2 = [None] * NT
        e_pe3 = [None] * NT
        e_a1 = [None] * NT
        e_a3 = [None] * NT
        o_free = [None, None]
        l_free = [None, None]
        stat_free = [None]
        deferred = {}
        epi_state = {"i": 0}

        def defer(step, fn):
            deferred.setdefault(step, []).append(fn)

        for step_, fn_ in bg:
            defer(step_, fn_)

        def pe1(i):
            h, G, n, kb, nkb = tiles[i]
            sl = i % nz
            Z = zb[sl]
            diag = kb >= 4 * G
            w = [z_free[sl]] + bar + [ev_c]
            if rnd == 0:
                ev = sc.op("pe", lambda e: e.matmul(
                    Z[:, :], kTs[rnd][:, h, 128 * kb:128 * kb + 128], qTs[rnd][:, h, 512 * G:512 * G + 512],
                    start=True, stop=(not diag)), w, signal=(not diag))
                if diag:
                    ev = sc.op("pe", lambda e: e.matmul(
                        Z[:, :], ident_bf, nm_sb(kb - 4 * G), start=False, stop=True), [st_m["ev"]])
            else:
                sc.op("pe", lambda e: e.matmul(
                    Z[:, :], kTs[rnd][:, h, 128 * kb:128 * kb + 128], qTs[rnd][:, h, 512 * G:512 * G + 512],
                    start=True, stop=False), w, signal=False)
                ev = sc.op("pe", lambda e: e.matmul(
                    Z[:, :], ones_bf, RT3[:, h, 512 * G:512 * G + 512],
                    start=False, stop=(not diag)), [st["rt3_ev"]], signal=(not diag))
                if diag:
                    ev = sc.op("pe", lambda e: e.matmul(
                        Z[:, :], ident_bf, nm_fx(kb - 4 * G), start=False, stop=True), [st_m["ev"]])
            e_pe1[i] = ev

        def act12(i):
            h, G, n, kb, nkb = tiles[i]
            sl = i % nz
            Z = zb[sl]
            a1 = sc.op("act", lambda e: e.activation(Eb[:, sl, :], Z[:, :], AF.Exp), [e_pe1[i]] + bar)
            prev = i - nz
            w = [a1]
            if prev >= 0:
                w += [e_pe2[prev], e_pool[prev]]
            a2 = sc.op("act", lambda e: e.activation(Lb[:, sl, :], Eb[:, sl, :], AF.Ln, bias=1.0), w)
            e_ln[i] = a2
            if n < nkb - 1:
                w = [a2] + bar
                if i >= 1:
                    w.append(e_pe2[i - 1])
                if n == 0:
                    e_pool[i] = sc.op("dve", lambda e: e.tensor_copy(out=LS[:, 1, :], in_=Lb[:, sl, :]), w)
                else:
                    e_pool[i] = sc.op("dve", lambda e: e.tensor_tensor(
                        LS[:, (n + 1) % 2, :], LS[:, n % 2, :], Lb[:, sl, :], ALU.add), w + [e_pool[i - 1]])

        def pe2(i):
            h, G, n, kb, nkb = tiles[i]
            sl = i % nz
            Z = zb[sl]
            ev = sc.op("pe", lambda e: e.matmul(Z[:, :], trineg, Lb[:, sl, :], start=False, stop=(n == 0),
                                                skip_group_check=True),
                       [e_ln[i]], signal=(n == 0))
            if n > 0:
                ev = sc.op("pe", lambda e: e.matmul(Z[:, :], onesneg, LS[:, n % 2, :], start=False, stop=True,
                                                    skip_group_check=True),
                           [e_pool[i - 1]])
            e_pe2[i] = ev

        def act3(i):
            sl = i % nz
            Z = zb[sl]
            prev = i - nz
            w = [e_pe2[i]]
            if prev >= 0:
                w.append(e_pe3[prev])
            e_a3[i] = sc.op("act", lambda e: e.activation(Ab[:, sl, :], Z[:, :], AF.Exp), w)
            z_free[sl] = e_a3[i]

        def act_fx(i):
            h, G, n, kb, nkb = tiles[i]
            sl = i % nz
            Z = zb[sl]
            prev = i - nz
            w = [e_pe1[i], st["cneg_ev"]] + bar
            if prev >= 0:
                w.append(e_pe3[prev])
            e_a3[i] = sc.op("act", lambda e: e.activation(
                Ab[:, sl, :], Z[:, :], AF.Exp, bias=cneg[:, 2 * kb + h:2 * kb + h + 1]), w)
            z_free[sl] = e_a3[i]

        def pe3(i, step):
            h, G, n, kb, nkb = tiles[i]
            sl = i % nz
            O = ob[G % 2]
            w = [e_a3[i]]
            if n == 0:
                w.append(o_free[G % 2])
            last = (n == nkb - 1)
            ev = sc.op("pe", lambda e: e.matmul(
                O[:, :], Vts[rnd][:, kb, 128 * h:128 * h + 128], Ab[:, sl, :], start=(n == 0), stop=last), w,
                signal=(rnd == 0 or last))
            ev_o = ev
            if rnd == 1:
                Lk = lb[G % 2]
                w2 = [l_free[G % 2]] if n == 0 else []
                ev = sc.op("pe", lambda e: e.matmul(
                    Lk[:, :], ones_bf, Ab[:, sl, :], start=(n == 0), stop=last), w2)
            e_pe3[i] = ev
            if last:
                epilogue(h, G, ev_o, step, ev)

        def epilogue(h, G, ev_o, step, ev_acc=None):
            hl = 2 * rnd + h
            O = ob[G % 2]
            k = epi_state["i"] % 2
            epi_state["i"] += 1
            src = O[:, :]
            src_ev = [ev_o]
            a1 = sc.op("act", lambda e: e.activation(sqo[:, k, :], O[:, :], AF.Square), [ev_o])
            a0 = None
            if rnd == 1:
                a0 = ev_acc

            def pe_stat(a1=a1, a0=a0, k=k, src=src, src_ev=src_ev, hl=hl, G=G, h=h):
                pm = sc.op("pe", lambda e: e.matmul(sb_[:, :], ones_bf, sqo[:, k, :], start=True, stop=True),
                           [a1, stat_free[0]])
                if rnd == 1:
                    Lk_ = lb[G % 2]
                    a0 = sc.op("act", lambda e: e.activation(
                        rl[:, k, :], Lk_[:, :], AF.Square, scale=EPS ** 0.5), [a0])
                    l_free[G % 2] = a0
                if rnd == 0:
                    d3 = sc.op("act", lambda e: e.activation(
                        rr[:, k, :], sb_[:, :], AF.Ln, bias=EPS, scale=1.0 / 128), [pm])
                    stat_free[0] = d3
                else:
                    d3a = sc.op("dve", lambda e: e.scalar_tensor_tensor(
                        rr[:, k, :], sb_[:, :], 1.0 / 128, rl[:, k, :], ALU.mult, ALU.add), [pm, a0])
                    stat_free[0] = d3a
                    d3 = sc.op("act", lambda e: e.activation(rr[:, k, :], rr[:, k, :], AF.Ln), [d3a])
                d4 = sc.op("act", lambda e: e.activation(rr[:, k, :], rr[:, k, :], AF.Exp, scale=-0.5), [d3])
                d5 = sc.op("dve", lambda e: e.scalar_tensor_tensor(
                    stg[:, k, :], src, gout[:, hl:hl + 1], rr[:, k, :], ALU.mult, ALU.mult),
                    [d4, stg_free[k]] + src_ev)
                o_free[G % 2] = d5
                if hl < 3:
                    dst_ap = cin[hl].ap()[:, 512 * G:512 * G + 512]
                elif G < 6:
                    dst_ap = cinA.ap()[:, 512 * G:512 * G + 512]
                else:
                    dst_ap = cinB.ap()[:, 512 * (G - 6):512 * (G - 6) + 512]
                dm = sc.dma("sp", stg_sem[k], lambda e: e.dma_start(out=dst_ap, in_=stg[:, k, :]), [d5])
                stg_free[k] = dm
                if G == 7 or (hl == 3 and G == 5):
                    if hl < 3:
                        gi, go = cin[hl], cout[hl]
                    elif G == 5:
                        gi, go = cinA, coutA
                    else:
                        gi, go = cinB, coutB

                    def gather(dm=dm, gi=gi, go=go, dmprev=[stg_free[1 - k]]):
                        st["cc_n"] += 1
                        n_ = st["cc_n"]
                        ws = sc._waits("pool", [dm, dmprev[0]])
                        sc.ops["pool"].append((lambda e: e.collective_compute(
                            "AllGather", ALU.bypass, replica_groups=[[0, 1, 2, 3], [4, 5, 6, 7]],
                            ins=[gi.ap().opt()], outs=[go.ap().opt()]), ws, (cc_sem, n_), 1))
                    if hl == 3 and G == 7:
                        gather()
                    else:
                        defer(step + 4, gather)
            defer(step + 3, pe_stat)

        nsteps = NT + 16
        pe1(0)
        for s_ in range(nsteps):
            for fn in deferred.pop(s_, []):
                fn()
            if rnd == 0:
                if 0 <= s_ - 1 < NT:
                    pe2(s_ - 1)
                if 0 <= s_ - 2 < NT:
                    pe3(s_ - 2, s_)
                if s_ + 1 < NT:
                    pe1(s_ + 1)
                if s_ < NT:
                    act12(s_)
                if 0 <= s_ - 1 < NT:
                    act3(s_ - 1)
                if filler and s_ >= filler_start:
                    span = max(1, NT - 38 - filler_start)
                    target = min(len(filler), ((s_ - filler_start + 1) * len(filler) + span - 1) // span)
                    while fpos[0] < target:
                        filler[fpos[0]]()
                        fpos[0] += 1
                if late and fpos[0] >= len(filler) and s_ >= NT - 34:
                    li = late_pos[0]
                    if li < len(late) and (s_ - (NT - 34)) >= late_steps[li]:
                        late[li]()
                        late_pos[0] += 1
            else:
                if 0 <= s_ - 1 < NT:
                    pe3(s_ - 1, s_)
                if s_ + 1 < NT:
                    pe1(s_ + 1)
                if s_ < NT:
                    act_fx(s_)
        for k_ in sorted(deferred):
            for fn in deferred[k_]:
                fn()
        while fpos[0] < len(filler):
            filler[fpos[0]]()
            fpos[0] += 1
        while late_pos[0] < len(late):
            late[late_pos[0]]()
            late_pos[0] += 1
        st["phase_bar"] = sc.all_last() + ([st["rt3_ev"]] if late else [])

    xo_sem = [sc.dsem("xo%d" % i) for i in range(4)]
    mx_sem = sc.dsem("mx")
    mx2_sem = sc.dsem("mx2")
    mxa_sem = sc.dsem("mxa")
    wo2_sem = sc.dsem("wo2")
    wo2a_sem = sc.dsem("wo2a")
    wo_sem = [sc.dsem("wo0"), sc.dsem("wo1")]

    def outproj():
        bar = list(st["phase_bar"]) + [stg_free[0], stg_free[1]]
        ccw = (cc_sem, 4)
        ccw5 = (cc_sem, 5)
        me = []

        def cv(t):
            return t.ap().rearrange("(r p) t -> p r t", p=128)
        cc3 = (cc_sem, 3)

        def mload(eng, sem, hl, part, src, wait):
            if part == 0:
                return sc.dma(eng, sem, lambda e: e.dma_start(
                    out=mixT[:, 4 * hl:4 * hl + 4, 0:768],
                    in_=cv(src)[:, :, bass.ds(gidx(eng) * 768, 768)]), [wait] + bar)
            off_ = 3072 if src.shape[1] == S else 0
            return sc.dma(eng, sem, lambda e: e.dma_start(
                out=mixT[:, 4 * hl:4 * hl + 4, 768:1024],
                in_=cv(src)[:, :, bass.ds(gidx(eng) * 256 + off_, 256)]), [wait] + bar)
        m_sp = [mload("sp", mx_sem, 0, 0, cout[0], cc3), mload("sp", mx_sem, 0, 1, cout[0], cc3),
                mload("sp", mx_sem, 2, 1, cout[2], cc3)]
        m_act = [mload("act", mxa_sem, 1, 0, cout[1], cc3), mload("act", mxa_sem, 1, 1, cout[1], cc3),
                 mload("act", mxa_sem, 2, 0, cout[2], cc3)]
        me = [m_sp[-1], m_act[-1]]
        xown_v = xown.rearrange("(c p) t -> p c t", p=128)
        xe = []
        for q4 in range(4):
            xe.append(sc.dma("sp", xo_sem[q4], lambda e, q4=q4: e.dma_start(
                out=x1T[:, 4 * q4:4 * q4 + 4, :], in_=xown_v[:, 4 * q4:4 * q4 + 4, :]), bar))
        me2 = [mload("sp", mx2_sem, 3, 0, coutA, ccw), mload("sp", mx2_sem, 3, 1, coutB, ccw5)]
        wout_v = wout.rearrange("(e p) d -> p e d", p=128)
        wo_free = [[], []]
        bank_free = [None] * 8
        sq_free = [None] * 4
        pend = []
        cnt = 0
        ss_ev = [None, None]
        we2a = sc.dma("pool", wo2a_sem, lambda e: e.dma_start(out=wo2a[:, :, :], in_=wout_v[:, 12:14, :]), bar)
        for dg in range(4):
            s_ = dg % 2
            if dg in st["wo_pre"]:
                we = st["wo_pre"][dg]
            else:
                we = sc.dma("pool", wo_sem[s_], lambda e, dg=dg, s_=s_: e.dma_start(
                    out=wo[:, s_, :, :], in_=wout_v[:, 0:12, 512 * dg:512 * dg + 512]), wo_free[s_] + bar)
            lastpe = None
            for dbl in range(4):
                db = 4 * dg + dbl
                for th in range(2):
                    bi = cnt % 3
                    cnt += 1
                    bk = PS[bi]
                    for e_ in range(12):
                        ev = sc.op("pe", lambda e, e_=e_, bk=bk, s_=s_, dbl=dbl, th=th: e.matmul(
                            bk[:, :], wo[:, s_, e_, 128 * dbl:128 * dbl + 128], mixT[:, e_, 512 * th:512 * th + 512],
                            start=(e_ == 0), stop=(e_ == 11)),
                            [we, me, bank_free[bi]] + bar, signal=(e_ == 11))
                    lastpe = ev
                    d1 = sc.op("dve", lambda e, bk=bk, db=db, th=th: e.tensor_tensor(
                        x1T[:, db, 512 * th:512 * th + 512], bk[:, :], x1T[:, db, 512 * th:512 * th + 512], ALU.add),
                        [ev, xe[db // 4]] + bar)
                    bank_free[bi] = d1
            wo_free[s_] = [lastpe]
        xn_ev = {}
        if st["wo2_pre"] is not None:
            we2b = st["wo2_pre"]
        else:
            we2b = sc.dma("pool", wo2_sem, lambda e: e.dma_start(out=wo2b[:, :, :], in_=wout_v[:, 14:16, :]), bar)
        we2 = [we2a, we2b]
        for db in range(16):
            for th in range(2):
                bi = cnt % 3
                bk = PS[bi]
                for e_ in range(4):
                    ev = sc.op("pe", lambda e, e_=e_, bk=bk, db=db, th=th: e.matmul(
                        bk[:, :], (wo2a if e_ < 2 else wo2b)[:, e_ % 2, 128 * db:128 * db + 128],
                        mixT[:, 12 + e_, 512 * th:512 * th + 512],
                        start=(e_ == 0), stop=(e_ == 3)),
                        [we2, me2[-1], bank_free[bi]] + bar, signal=(e_ == 3))
                d1 = sc.op("dve", lambda e, bk=bk, db=db, th=th: e.tensor_tensor(
                    x1T[:, db, 512 * th:512 * th + 512], bk[:, :], x1T[:, db, 512 * th:512 * th + 512], ALU.add),
                    [ev])
                bank_free[bi] = d1
                k = cnt % 4
                a1 = sc.op("act", lambda e, db=db, th=th, k=k: e.activation(
                    sq2[:, k, :], x1T[:, db, 512 * th:512 * th + 512], AF.Square), [d1, sq_free[k]])
                xn_ev[(db, th)] = sc.op("act", lambda e, db=db, th=th: e.activation(
                    x1n[:, db, 512 * th:512 * th + 512], x1T[:, db, 512 * th:512 * th + 512], AF.Copy,
                    scale=gmlp[:, db:db + 1]), [d1])
                for fn in pend:
                    fn()
                pend = []

                def stat(a1=a1, k=k, th=th, db=db):
                    ev2 = sc.op("pe", lambda e: e.matmul(
                        PS[6 + th][:, :], ones_bf, sq2[:, k, :], start=(db == 0), stop=(db == 15)), [a1] + ev_c)
                    sq_free[k] = ev2
                    ss_ev[th] = ev2
                pend.append(stat)
                cnt += 1
        for fn in pend:
            fn()
        nx = []
        for th in range(2):
            d1 = sc.op("act", lambda e, th=th: e.activation(
                r2bc[:, 512 * th:512 * th + 512], PS[6 + th][:, :], AF.Ln, bias=EPS, scale=1.0 / D), [ss_ev[th]])
            d2 = sc.op("act", lambda e, th=th: e.activation(
                r2bc[:, 512 * th:512 * th + 512], r2bc[:, 512 * th:512 * th + 512], AF.Exp, scale=-1.0), [d1])
            nx.append(d2)
        st["r2_ev"] = nx
        st["xn_ev"] = xn_ev
        st["phase_bar"] = sc.all_last()

    wu_sem = [sc.dsem("wu0"), sc.dsem("wu1")]
    wd_sem = [sc.dsem("wd%d" % i) for i in range(3)]

    def ffn():
        bar = list(st["phase_bar"])
        xn_ev = st["xn_ev"]
        wup_v = wup.rearrange("(c p) f -> p c f", p=128)
        wdn_v = wdn.rearrange("(fb p) d -> p fb d", p=128)
        wu_free = [[], []]
        wd_free = [[], [], []]
        ubank = [PS[0], PS[1], PS[2]]
        dbank = [PS[3], PS[4], PS[5]]
        u_free = [None] * 3
        tmp_free = [None, None]
        d_free = [None] * 3
        rT_w = {}
        rT_free = []
        ucnt = 0
        dcnt = 0
        wucnt = 0
        wdcnt = 0
        last_evac = None
        fcnt = [0]
        fsq_free = [None] * 4
        fpend = []
        st["fin_ss"] = [None, None]
        for fg in range(8):
            new_rT_free = []
            for sg in range(2):
                s_ = wucnt % 2
                wucnt += 1
                f0 = 1024 * fg + 512 * sg
                if wucnt == 1 and st["wu_pre"] is not None:
                    we = st["wu_pre"]
                else:
                    we = sc.dma("pool", wu_sem[s_], lambda e, s_=s_, f0=f0: e.dma_start(
                        out=wu_t[s_][:, :, :], in_=wup_v[:, :, f0:f0 + 512]), wu_free[s_] + bar)
                lastpe = None
                for fbl in range(4):
                    fb = 4 * sg + fbl
                    for th in range(2):
                        bi = ucnt % 3
                        ucnt += 1
                        bk = ubank[bi]
                        for c in range(NCH):
                            ev = sc.op("pe", lambda e, c=c, bk=bk, s_=s_, fbl=fbl, th=th: e.matmul(
                                bk[:, :], wu_t[s_][:, c, 128 * fbl:128 * fbl + 128], x1n[:, c, 512 * th:512 * th + 512],
                                start=(c == 0), stop=(c == NCH - 1)),
                                [we, xn_ev[(c, th)], u_free[bi]] + bar, signal=(c == NCH - 1))
                        lastpe = ev
                        k = (ucnt - 1) % 2
                        a1 = sc.op("act", lambda e, bk=bk, k=k: e.activation(rtmp[:, k, :], bk[:, :], AF.Relu),
                                   [ev, tmp_free[k]] + bar)
                        u_free[bi] = a1
                        a2 = sc.op("act", lambda e, k=k: e.activation(rtmp[:, k, :], rtmp[:, k, :], AF.Square), [a1])
                        d1 = sc.op("dve", lambda e, k=k, fb=fb, th=th: e.tensor_tensor(
                            rT[:, fb, 512 * th:512 * th + 512], rtmp[:, k, :], r2bc[:, 512 * th:512 * th + 512],
                            ALU.mult), [a2, st["r2_ev"][th]] + rT_free + bar)
                        tmp_free[k] = d1
                        rT_w[(fb, th)] = d1
                wu_free[s_] = [lastpe]
            for dg in range(4):
                s_ = wdcnt % 3
                wdcnt += 1
                we = sc.dma("pool", wd_sem[s_], lambda e, s_=s_, fg=fg, dg=dg: e.dma_start(
                    out=wd[:, s_, :, :], in_=wdn_v[:, 8 * fg:8 * fg + 8, 512 * dg:512 * dg + 512]),
                    wd_free[s_] + bar)
                lastpe = None
                for dbl in range(4):
                    db = 4 * dg + dbl
                    for th in range(2):
                        bi = dcnt % 3
                        dcnt += 1
                        bk = dbank[bi]
                        for fb in range(8):
                            ev = sc.op("pe", lambda e, fb=fb, bk=bk, s_=s_, dbl=dbl, th=th: e.matmul(
                                bk[:, :], wd[:, s_, fb, 128 * dbl:128 * dbl + 128], rT[:, fb, 512 * th:512 * th + 512],
                                start=(fb == 0), stop=(fb == 7)),
                                [we, rT_w[(fb, th)], d_free[bi]], signal=(fb == 7))
                        lastpe = ev
                        d1 = sc.op("dve", lambda e, bk=bk, db=db, th=th: e.tensor_tensor(
                            x1T[:, db, 512 * th:512 * th + 512], bk[:, :], x1T[:, db, 512 * th:512 * th + 512],
                            ALU.add), [ev])
                        d_free[bi] = d1
                        last_evac = d1
                        if fg == 7:
                            k = fcnt[0] % 4
                            fcnt[0] += 1
                            a1 = sc.op("act", lambda e, db=db, th=th, k=k: e.activation(
                                sq2[:, k, :], x1T[:, db, 512 * th:512 * th + 512], AF.Square), [d1, fsq_free[k]])

                            def fstat(a1=a1, k=k, th=th, db=db):
                                ev2 = sc.op("pe", lambda e: e.matmul(
                                    PS[6 + th][:, :], ones_bf, sq2[:, k, :], start=(db == 0), stop=(db == 15)),
                                    [a1] + ev_c)
                                fsq_free[k] = ev2
                                st["fin_ss"][th] = ev2
                            fpend.append(fstat)
                            while len(fpend) > 2:
                                fpend.pop(0)()
                wd_free[s_] = [lastpe]
                new_rT_free = [lastpe]
            rT_free = new_rT_free
        for fn in fpend:
            fn()
        st["phase_bar"] = sc.all_last()

    out_sem = sc.dsem("out")

    def final():
        bar = list(st["phase_bar"])
        ss_ev = st["fin_ss"]
        nx = []
        for th in range(2):
            d1 = sc.op("act", lambda e, th=th: e.activation(
                r2bc[:, 512 * th:512 * th + 512], PS[6 + th][:, :], AF.Ln, bias=EPS, scale=1.0 / D),
                [ss_ev[th]] + bar)
            d2 = sc.op("act", lambda e, th=th: e.activation(
                r2bc[:, 512 * th:512 * th + 512], r2bc[:, 512 * th:512 * th + 512], AF.Exp, scale=-0.5), [d1])
            nx.append(d2)
        yT_v = yT.rearrange("(c p) t -> p c t", p=128)
        evs = []
        for c in range(NCH):
            eng = "dve"
            d = sc.op(eng, lambda e, c=c: e.scalar_tensor_tensor(
                x1T[:, c, :], x1T[:, c, :], gfin[:, c:c + 1], r2bc[:, :], ALU.mult, ALU.mult),
                [nx[0], nx[1]] + bar)
            evs.append(d)
            if c % 4 == 3:
                st["out_ev"] = sc.dma("sp", out_sem, lambda e, c=c: e.dma_start(
                    out=yT_v[:, c - 3:c + 1, :], in_=x1T[:, c - 3:c + 1, :]), evs[-4:])

    st["wo_pre"] = {}
    st["wu_pre"] = None
    st["wo2_pre"] = None

    def prefetch_jobs():
        wout_v = wout.rearrange("(e p) d -> p e d", p=128)
        wup_v = wup.rearrange("(c p) f -> p c f", p=128)
        war = list(st["att0_end"])

        def wo_job(dg):
            st["wo_pre"][dg] = sc.dma("pool", wo_sem[dg], lambda e: e.dma_start(
                out=wo[:, dg, :, :], in_=wout_v[:, 0:12, 512 * dg:512 * dg + 512]), war)

        def wu_job():
            st["wu_pre"] = sc.dma("pool", wu_sem[0], lambda e: e.dma_start(
                out=wu0[:, :, :], in_=wup_v[:, :, 0:512]), war)
        def wo2_job():
            st["wo2_pre"] = sc.dma("pool", wo2_sem, lambda e: e.dma_start(
                out=wo2b[:, :, :], in_=wout_v[:, 14:16, :]), war)
        return [(10, lambda: wo_job(0)), (70, lambda: wo_job(1)), (130, wo2_job), (170, wu_job)]

    st["bg0"] = {}
    for kind, fn in inproj_prep(0):
        fn()
    if stage >= 3:
        jobs1 = inproj_prep(1)
        tile_of = 1
        cnt_ = 0
        for kind, fn in jobs1:
            if kind == "w":
                fn()
            elif kind == "s":
                st["bg0"].setdefault(tile_of, []).append(fn)
                cnt_ += 1
                if cnt_ % 3 == 0:
                    tile_of += 1
            else:
                st["bg0"].setdefault(6, []).append(fn)
    inproj(0)
    for j_ in sorted(st["bg0"]):
        for fn in st["bg0"][j_]:
            fn()
    st["bg0"] = {}
    st["bar0"] = list(st["phase_bar"])
    if stage >= 2:
        items = []
        if stage >= 3:
            items = inproj1_items()
        attention(0, [], items, 2, fox_c_stages() if stage >= 3 else ())
        st["att0_end"] = list(st["phase_bar"])
    if stage >= 4:
        attention(1, prefetch_jobs() if stage >= 5 else ())
    if stage >= 5:
        outproj()
    if stage >= 6:
        ffn()
        final()

    dump_sem = sc.dsem("dump")
    dump_ev = None
    name2t = {"qT": (qTs[1], [128, 2, S], BF16), "kT": (kTs[1], [128, 2, S], BF16), "Vt": (Vts[1], [128, 32, 256], BF16),
              "rstd_bc": (rstd_bc, [128, S], F32), "rstd_col": (rstd_col, [128, 32], F32),
              "cneg": (cneg, [128, 64], F32), "RT3": (RT3, [128, 2, S], BF16),
              "x1T": (x1T, [128, 16, TOK], F32), "mixT": (mixT, [128, 16, TOK], BF16),
              "x1n": (x1n, [128, 16, TOK], BF16), "r2bc": (r2bc, [128, TOK], F32), "fz": (fz, [128, 64], F32)}
    for nm in dumps:
        bar = sc.all_last()
        if nm.startswith("cin"):
            hl = int(nm[3:])
            dt_ = nc.dram_tensor("dbg_" + nm, [128, S], BF16, kind="ExternalOutput").ap()
            dump_ev = sc.dma("sp", dump_sem, lambda e, dt_=dt_, hl=hl: e.dma_start(out=dt_, in_=cin[hl].ap()),
                             bar + [(stg_sem[0]["h"], stg_sem[0]["n"]), (stg_sem[1]["h"], stg_sem[1]["n"])])
            continue
        t, shp, dty = name2t[nm]
        dt_ = nc.dram_tensor("dbg_" + nm, shp, dty, kind="ExternalOutput").ap()
        if len(shp) == 3:
            dump_ev = sc.dma("sp", dump_sem, lambda e, dt_=dt_, t=t: e.dma_start(out=dt_, in_=t[:, :, :]), bar)
        else:
            dump_ev = sc.dma("sp", dump_sem, lambda e, dt_=dt_, t=t: e.dma_start(out=dt_, in_=t[:, :]), bar)

    fin = []
    if stage >= 6:
        fin.append(st["out_ev"])
    if dump_ev is not None:
        fin.append(dump_ev)
    if st["cc_n"] > 0:
        fin.append((cc_sem, st["cc_n"]))
    fin += sc.all_last()
    for d_ in [ld_c, ld_cb, ld_cm, mx_sem, mx2_sem, mxa_sem, wo2_sem, wo2a_sem] + xo_sem + x_sem[0] + x_sem[1] + x2_sem + w_sem + w2_sem + wo_sem + wu_sem + wd_sem + stg_sem:
        if d_["n"] > 0:
            fin.append((d_["h"], d_["n"]))
    ws = sc._waits("sp", fin)
    sc.ops["sp"].append((lambda e: e.engine_nop() if hasattr(e, "engine_nop") else None, ws, None, 0))

    with nc.Block() as block:
        @block.tensor
        def _(e):
            sc.replay("pe", e)

        @block.scalar
        def _(e):
            if stage >= 5:
                gidx("act")
            sc.replay("act", e)

        @block.vector
        def _(e):
            sc.replay("dve", e)

        @block.gpsimd
        def _(e):
            sc.replay("pool", e)

        @block.sync
        def _(e):
            if stage >= 5:
                gidx("sp")
            for fn, ws_, ev, inc in sc.ops["sp"]:
                for sem, val in ws_:
                    e.wait_ge(sem, val)
                if inc == 0:
                    continue
                ins = fn(e)
                if ev is not None:
                    ins.then_inc(ev[0], inc)
    return nc


def _own_tokens(g):
    return np.concatenate([np.arange(768 * g, 768 * g + 768), np.arange(3072 + 256 * g, 3072 + 256 * g + 256)])


def _consts():
    p = np.arange(128)
    ident = np.eye(128, dtype=np.float32)
    ones = np.ones((128, 128), np.float32)
    trineg = -(p[:, None] >= p[None, :]).astype(np.float32)
    c = np.arange(512)
    nm_sb = np.stack([np.where(128 * j + p[:, None] >= c[None, :], NEG, 0.0) for j in range(4)], 1)
    nm_fx = np.stack([np.where(128 * j + p[:, None] > c[None, :], NEG, 0.0) for j in range(4)], 1)
    cbf = np.concatenate([ident, ones, trineg, -ones, nm_sb.reshape(128, -1), nm_fx.reshape(128, -1)], 1)
    triincl = (p[:, None] <= p[None, :]).astype(np.float32)
    e0 = np.zeros((128, 1), np.float32)
    e0[0, 0] = 1.0
    return cbf.astype(np.float32), triincl, ones, ident, e0


def _prep_inputs(x, g_attn, w_in, b_f, g_out_sb, g_out_fox, w_out, g_mlp, w_up, w_down, g_final):
    x = np.asarray(x, np.float32)
    w_in = np.asarray(w_in, np.float32)[0]
    w_out = np.asarray(w_out, np.float32)[0]
    w_up = np.ascontiguousarray(np.asarray(w_up, np.float32)[0])
    w_down = np.ascontiguousarray(np.asarray(w_down, np.float32)[0])
    g_attn = np.asarray(g_attn, np.float32)[0]
    b_f = np.asarray(b_f, np.float32)[0]
    g_out_sb = np.asarray(g_out_sb, np.float32)[0]
    g_out_fox = np.asarray(g_out_fox, np.float32)[0]
    g_mlp = np.asarray(g_mlp, np.float32)[0]
    g_final = np.asarray(g_final, np.float32)
    cbf, triincl, ones, ident, e0 = _consts()

    def col16(v):
        return np.ascontiguousarray(v.reshape(16, 128).T)

    xTs = [np.ascontiguousarray(x[b].T) for b in range(2)]
    in_maps = []
    for core in range(8):
        b, g = core // 4, core % 4
        h0, h1 = 2 * g, 2 * g + 1

        def hc(base, h):
            return list(range(base + 128 * h, base + 128 * h + 128))
        cols_sb = hc(0, h0) + hc(0, h1) + hc(1024, h0) + hc(1024, h1) + hc(2048, h0) + hc(2048, h1)
        cols_fx = (hc(3072, h0) + hc(3072, h1) + hc(4096, h0) + hc(4096, h1) + hc(5120, h0) + hc(5120, h1)
                   + [6144 + h0, 6144 + h1])
        rows = []
        for hl in range(4):
            for r in range(4):
                if hl < 2:
                    hd = 2 * r + hl
                    rows += list(range(128 * hd, 128 * hd + 128))
                else:
                    hd = 2 * r + hl - 2
                    rows += list(range(1024 + 128 * hd, 1024 + 128 * hd + 128))
        gout = np.stack([g_out_sb[128 * h0:128 * h0 + 128], g_out_sb[128 * h1:128 * h1 + 128],
                         g_out_fox[128 * h0:128 * h0 + 128], g_out_fox[128 * h1:128 * h1 + 128]], 1)
        bfb = np.broadcast_to(b_f[[h0, h1]][None, :], (128, 2))
        cf = np.concatenate([triincl, ones, ident, e0, col16(g_attn), bfb, gout, col16(g_mlp), col16(g_final)],
                            1).astype(np.float32)
        in_maps.append({
            "xT": xTs[b],
            "xown": np.ascontiguousarray(xTs[b][:, _own_tokens(g)]),
            "win_sb": np.ascontiguousarray(w_in[:, cols_sb]),
            "win_fx": np.ascontiguousarray(w_in[:, cols_fx]),
            "wout": np.ascontiguousarray(w_out[rows, :]),
            "wup": w_up,
            "wdn": w_down,
            "cbf": cbf,
            "cf": np.ascontiguousarray(cf),
        })
    return in_maps


_NC_CACHE = {}


def kernel(x, g_attn, w_in, b_f, g_out_sb, g_out_fox, w_out, g_mlp, w_up, w_down, g_final):
    in_maps = _prep_inputs(x, g_attn, w_in, b_f, g_out_sb, g_out_fox, w_out, g_mlp, w_up, w_down, g_final)
    if "nc" not in _NC_CACHE:
        _NC_CACHE["nc"] = _build()
    nc = _NC_CACHE["nc"]
    res = run_bass_kernel_spmd(nc, in_maps, core_ids=list(range(8)))
    y = np.empty((2, S, D), np.float32)
    for core in range(8):
        b, g = core // 4, core % 4
        y[b, _own_tokens(g), :] = res.results[core]["yT"].T
    return y
```

```python
import numpy as np
import concourse.bass as bass
import concourse.mybir as mybir
from concourse.bass_utils import run_bass_kernel_spmd

F32 = mybir.dt.float32
BF16 = mybir.dt.bfloat16
ALU = mybir.AluOpType
AF = mybir.ActivationFunctionType

D = 2048
S = 4096
NCH = 16
TOK = 1024
DFF = 8192
EPS = 1e-6
SCALE = 128 ** -0.5
NEG = -32768.0

ENGS = ["pe", "act", "dve", "pool", "sp"]


class Sched:
    def __init__(self, nc):
        self.nc = nc
        self.ops = {e: [] for e in ENGS}
        self.sem = {}
        self.cnt = {e: 0 for e in ENGS}
        self.waited = {e: {} for e in ENGS}
        self.last = {e: None for e in ENGS}
        self.nsem = 0

    def new_sem(self, name):
        h = self.nc.alloc_semaphore(name)
        self.nsem += 1
        return h

    def init(self):
        for e in ENGS:
            self.sem[e] = self.new_sem("prog_" + e)

    def _waits(self, eng, waits):
        ws = []
        for w in waits:
            if w is None:
                continue
            if isinstance(w, list):
                ws.extend(self._waits(eng, w))
                continue
            sem, val = w
            key = sem.name if hasattr(sem, "name") else id(sem)
            if self.waited[eng].get(key, 0) >= val:
                continue
            self.waited[eng][key] = val
            ws.append((sem, val))
        return ws

    def op(self, eng, fn, waits=(), signal=True):
        ws = self._waits(eng, waits)
        ev = None
        if signal:
            self.cnt[eng] += 1
            ev = (self.sem[eng], self.cnt[eng])
            self.last[eng] = ev
        self.ops[eng].append((fn, ws, ev, 1))
        return ev

    def dma(self, eng, dsem, fn, waits=()):
        ws = self._waits(eng, waits)
        dsem["n"] += 16
        ev = (dsem["h"], dsem["n"])
        self.ops[eng].append((fn, ws, ev, 16))
        return ev

    def dsem(self, name):
        return {"h": self.new_sem(name), "n": 0}

    def all_last(self):
        return [self.last[e] for e in ENGS if self.last[e] is not None]

    def replay(self, eng, e):
        for fn, ws, ev, inc in self.ops[eng]:
            for sem, val in ws:
                e.wait_ge(sem, val)
            ins = fn(e)
            if ev is not None:
                ins.then_inc(ev[0], inc)


def _build(stage=99, dumps=()):
    nc = bass.Bass("TRN2", target_bir_lowering=False)
    sc = Sched(nc)
    sc.init()

    def din(name, shape, dt=F32):
        return nc.dram_tensor(name, list(shape), dt, kind="ExternalInput").ap()

    xT = din("xT", [D, S])
    xown = din("xown", [D, TOK])
    win_sb = din("win_sb", [D, 768])
    win_fx = din("win_fx", [D, 770])
    wout = din("wout", [D, D])
    wup = din("wup", [D, DFF])
    wdn = din("wdn", [DFF, D])
    NCB = 512 + 4096
    NCF = 128 * 3 + 1 + 16 + 2 + 4 + 16 + 16
    cbf_d = din("cbf", [128, NCB])
    cf_d = din("cf", [128, NCF])
    yT = nc.dram_tensor("yT", [D, TOK], F32, kind="ExternalOutput").ap()

    cin = [nc.dram_tensor("cin%d" % h, [128, S], BF16) for h in range(4)]
    cout = [nc.dram_tensor("cout%d" % h, [512, S], BF16) for h in range(4)]
    dscr = [nc.dram_tensor("dscr%d" % h, [96, 128], BF16) for h in range(2)]
    cinA = nc.dram_tensor("cinA", [128, 3072], BF16)
    coutA = nc.dram_tensor("coutA", [512, 3072], BF16)
    cinB = nc.dram_tensor("cinB", [128, 1024], BF16)
    coutB = nc.dram_tensor("coutB", [512, 1024], BF16)

    BASE = 16640
    LIMIT = 229376
    off = {"v": BASE}

    def alloc(name, shape, dt, at=None):
        nb = int(np.prod(shape[1:])) * (4 if dt == F32 else 2)
        nb = (nb + 63) // 64 * 64
        if at is None:
            at = off["v"]
            off["v"] = at + nb
        assert at + nb <= LIMIT, (name, at, nb)
        return nc.alloc_sbuf_tensor_at(name, list(shape), dt, offset=at), at + nb

    cbf, _ = alloc("cbf_sb", [128, NCB], BF16)
    cf, _ = alloc("cf_sb", [128, NCF], F32)
    gs, _ = alloc("gs", [128, 16], F32)
    A1 = off["v"]

    ident_bf = cbf[:, 0:128]
    ones_bf = cbf[:, 128:256]
    trineg = cbf[:, 256:384]
    onesneg = cbf[:, 384:512]

    def nm_sb(j):
        return cbf[:, 512 + 512 * j: 512 + 512 * (j + 1)]

    def nm_fx(j):
        return cbf[:, 2560 + 512 * j: 2560 + 512 * (j + 1)]

    triincl_f = cf[:, 0:128]
    ones_f = cf[:, 128:256]
    ident_f = cf[:, 256:384]
    e0 = cf[:, 384:385]
    gattn = cf[:, 385:401]
    bf_bc = cf[:, 401:403]
    gout = cf[:, 403:407]
    gmlp = cf[:, 407:423]
    gfin = cf[:, 423:439]

    o = A1
    rstd_bc, o = alloc("rstd_bc", [128, S], F32, o)
    rstd_col, o = alloc("rstd_col", [128, 32], F32, o)
    S0 = o
    qTs, kTs, Vts = [None, None], [None, None], [None, None]
    qTs[0], o = alloc("qT0", [128, 2, S], BF16, o)
    kTs[0], o = alloc("kT0", [128, 2, S], BF16, o)
    Vts[0], o = alloc("Vt0", [128, 32, 256], BF16, o)
    F0 = o
    qTs[1], o = alloc("qT1", [128, 2, S], BF16, o)
    kTs[1], o = alloc("kT1", [128, 2, S], BF16, o)
    Vts[1], o = alloc("Vt1", [128, 32, 256], BF16, o)
    xb, o2 = alloc("xb", [128, 2, 16, 512], BF16, F0)
    sq, o2 = alloc("sq", [128, 16, 512], BF16, o2)
    assert o2 <= o
    fz, o = alloc("fz", [128, 64], F32, o)
    spt, o = alloc("spt", [128, 64], F32, o)
    acc, o = alloc("acc", [128, 64], F32, o)
    cneg, o = alloc("cneg", [128, 64], F32, o)
    hib, o = alloc("hib", [128, 64], BF16, o)
    hif, o = alloc("hif", [128, 64], F32, o)
    r1, o = alloc("r1", [128, 64], F32, o)
    midb, o = alloc("midb", [128, 64], BF16, o)
    midf, o = alloc("midf", [128, 64], F32, o)
    r2, o = alloc("r2", [128, 64], F32, o)
    lob, o = alloc("lob", [128, 64], BF16, o)
    Xh, o = alloc("Xh", [128, 2, 96], F32, o)
    trs, o = alloc("trs", [128, 2, 128], BF16, o)
    P1 = o
    wbf, o = alloc("wbf", [128, 16, 770], BF16, P1)
    RT3, _ = alloc("RT3", [128, 2, S], BF16, P1)
    xb2, o = alloc("xb2", [128, 2, 16, 256], BF16, o)
    P2 = o
    NZ = 4
    Eb, o = alloc("Eb", [128, NZ, 512], F32, P2)
    Lb, o = alloc("Lb", [128, NZ, 512], BF16, o)
    Ab, o = alloc("Ab", [128, NZ, 512], BF16, o)
    LS, o = alloc("LS", [128, 2, 512], BF16, o)
    sqo, o = alloc("sqo", [128, 2, 512], BF16, o)
    rr, o = alloc("rr", [128, 2, 512], F32, o)
    rl, o = alloc("rl", [128, 2, 512], F32, o)
    stg, o = alloc("stg", [128, 2, 512], BF16, o)

    o = A1
    r2bc, o = alloc("r2bc", [128, TOK], F32, o)
    sq2, o = alloc("sq2", [128, 4, 512], BF16, o)
    wo2a, o = alloc("wo2a", [128, 2, D], BF16, o)
    assert o <= S0
    wu0, o = alloc("wu0", [128, 16, 512], BF16, S0)
    wo, o = alloc("wo", [128, 2, 12, 512], BF16, o)
    wo2b, o = alloc("wo2b", [128, 2, D], BF16, o)
    assert o <= F0
    x1T, o = alloc("x1T", [128, 16, TOK], F32, F0)
    x1n, o = alloc("x1n", [128, 16, TOK], BF16, o)
    PM = o
    mixT, o = alloc("mixT", [128, 16, TOK], BF16, PM)
    wu1, o = alloc("wu1", [128, 16, 512], BF16, S0 + 16384)
    rT, o = alloc("rT", [128, 8, TOK], BF16, o)
    assert o <= F0
    wd, o = alloc("wd", [128, 3, 8, 512], BF16, PM)
    rtmp, o = alloc("rtmp", [128, 2, 512], F32, o)
    wu_t = [wu0, wu1]

    PS = [nc.alloc_psum_tensor("ps%d" % i, [128, 512], F32) for i in range(8)]

    ld_c = sc.dsem("ld_c")
    gcache = {}
    _et = {"sp": mybir.EngineType.SP, "act": mybir.EngineType.Activation}

    def gidx(eng="sp"):
        if eng not in gcache:
            gcache[eng] = nc.partition_id([_et[eng]]) % 4
        return gcache[eng]
    ld_cb = sc.dsem("ld_cb")

    ev_cbf = sc.dma("pool", ld_cb, lambda e: e.dma_start(out=cbf[:, 0:512], in_=cbf_d[:, 0:512]))
    ld_cm = sc.dsem("ld_cm")
    st_m = {}

    def load_masks():
        st_m["ev"] = sc.dma("pool", ld_cm, lambda e: e.dma_start(out=cbf[:, 512:NCB], in_=cbf_d[:, 512:NCB]))
    ev_cf = sc.dma("sp", ld_c, lambda e: e.dma_start(out=cf[:, :], in_=cf_d))
    ev_c = [ev_cf, ev_cbf]
    ev_gs = sc.op("dve", lambda e: e.tensor_scalar(gs[:, :], gattn, SCALE, None, ALU.mult), ev_c)

    xT_v = xT.rearrange("(c p) t -> p c t", p=128)

    x_sem = [[sc.dsem("x%d%d" % (a, b_)) for b_ in range(2)] for a in range(2)]
    w_sem = [sc.dsem("w%d" % q) for q in range(4)]
    st = {"xb_free": [[], []], "sq_free": [], "bank_free": [None] * 8,
          "phase_bar": [], "wev": {}, "xe": {}, "pe_last_inproj": None}

    def x_tile_dma(j, waits):
        s_ = j % 2
        t0 = 512 * j
        xe = []
        for hf in range(2):
            xe.append(sc.dma("pool", x_sem[s_][hf],
                             lambda e, hf=hf, s_=s_, t0=t0: e.dma_start(
                                 out=xb[:, s_, 8 * hf:8 * hf + 8, :],
                                 in_=xT_v[:, 8 * hf:8 * hf + 8, t0:t0 + 512]),
                             st["xb_free"][s_] + waits))
        return xe

    x2_sem = [sc.dsem("x2_0"), sc.dsem("x2_1")]
    st["x2e"] = {}
    st["x2_free"] = [[], []]

    def x2_tile_dma(t):
        s_ = t % 2
        return sc.dma("pool", x2_sem[s_], lambda e: e.dma_start(
            out=xb2[:, s_, :, :], in_=xT_v[:, :, 256 * t:256 * t + 256]), st["x2_free"][s_])

    def inproj1_items():
        ncol, nv = 770, 258
        banks = [PS[3], PS[6]]
        bfree = [list(st["bar0"]), list(st["bar0"])]
        gc = [0]
        items = []

        def group(kind, t, idx):
            s_ = t % 2
            t0 = 256 * t
            gi = gc[0] % 2
            gc[0] += 1
            bk = banks[gi]
            state = {}

            def chunk(c0):
                wev = st["wev"][1]
                xe = st["x2e"][t]
                for c in range(c0, c0 + 4):
                    if kind == "qk":
                        ev = sc.op("pe", lambda e, c=c: e.matmul(
                            bk[:, 0:256], wbf[:, c, 128 * idx:128 * idx + 128], xb2[:, s_, c, :],
                            start=(c == 0), stop=(c == NCH - 1)),
                            [xe, wev[c][0], wev[c][1]] + bfree[gi], signal=(c == NCH - 1))
                    else:
                        ev = sc.op("pe", lambda e, c=c: e.matmul(
                            bk[:, 0:nv], xb2[:, s_, c, 128 * idx:128 * idx + 128], wbf[:, c, 512:ncol],
                            start=(c == 0), stop=(c == NCH - 1)),
                            [xe, wev[c][1]] + bfree[gi], signal=(c == NCH - 1))
                if c0 + 4 == NCH:
                    if kind == "qk":
                        dst = qTs[1] if idx < 2 else kTs[1]
                        e1 = sc.op("dve", lambda e: e.tensor_tensor(
                            dst[:, idx % 2, t0:t0 + 256], bk[:, 0:256], rstd_bc[:, t0:t0 + 256], ALU.mult), [ev])
                        bfree[gi] = [e1]
                    else:
                        blk = 2 * t + idx
                        e1 = sc.op("dve", lambda e: e.tensor_scalar(
                            Vts[1][:, blk, :], bk[:, 0:256], rstd_col[:, blk:blk + 1], None, ALU.mult), [ev])
                        e2 = sc.op("dve", lambda e: e.scalar_tensor_tensor(
                            fz[:, 2 * blk:2 * blk + 2], bk[:, 256:258], rstd_col[:, blk:blk + 1], bf_bc,
                            ALU.mult, ALU.add), [ev] + ev_c)
                        bfree[gi] = [e1, e2]
                        st["fz_last"] = e2
                        if idx == 1:
                            st["x2_free"][s_] = [ev]
                            st["pe_last_inproj"] = ev
            for c0 in range(0, NCH, 4):
                items.append(lambda c0=c0: chunk(c0))

        for t in range(16):
            if t + 1 < 16 and t + 1 >= 2:
                items.append(lambda t=t: st["x2e"].__setitem__(t + 1, x2_tile_dma(t + 1)))
            for eb in range(4):
                group("qk", t, eb)
            for i in range(2):
                group("v", t, i)
        return items

    def inproj_prep(rnd):
        win = win_sb if rnd == 0 else win_fx
        ncol = 768 if rnd == 0 else 770
        win_v = win.rearrange("(c p) n -> p c n", p=128)
        war = [st["pe_last_inproj"]]
        wdma = [None] * 4
        st["wev"][rnd] = [None] * NCH
        st["xe"][rnd] = {}
        jobs = []

        def wjob(q):
            wdma[q] = sc.dma("pool", w_sem[q], lambda e: e.dma_start(
                out=wbf[:, 4 * q:4 * q + 4, 0:ncol], in_=win_v[:, 4 * q:4 * q + 4, :]), war)

        def xjob(j):
            if rnd == 0:
                st["xe"][rnd][j] = x_tile_dma(j, [])
            else:
                st["x2e"][j] = x2_tile_dma(j)

        def sjob(c):
            eng = "dve" if (rnd == 1 or c % 2 == 0) else "act"
            w = [wdma[c // 4], ev_gs] + ev_c
            if eng == "dve":
                e1 = sc.op(eng, lambda e: e.tensor_scalar(
                    wbf[:, c, 0:256], wbf[:, c, 0:256], gs[:, c:c + 1], None, ALU.mult), w)
                e2 = sc.op(eng, lambda e: e.tensor_scalar(
                    wbf[:, c, 256:ncol], wbf[:, c, 256:ncol], gattn[:, c:c + 1], None, ALU.mult), w)
            else:
                e1 = sc.op(eng, lambda e: e.activation(
                    wbf[:, c, 0:256], wbf[:, c, 0:256], AF.Copy, scale=gs[:, c:c + 1]), w)
                e2 = sc.op(eng, lambda e: e.activation(
                    wbf[:, c, 256:ncol], wbf[:, c, 256:ncol], AF.Copy, scale=gattn[:, c:c + 1]), w)
            st["wev"][rnd][c] = [e1, e2]

        if rnd == 0:
            jobs.append(("x", lambda: xjob(0)))
        for q in range(4):
            jobs.append(("w", lambda q=q: wjob(q)))
        if rnd == 0:
            jobs.append(("m", load_masks))
            jobs.append(("x", lambda: xjob(1)))
        else:
            for j in range(2):
                jobs.append(("x", lambda j=j: xjob(j)))
        for c in range(NCH):
            jobs.append(("s", lambda c=c: sjob(c)))
        return jobs

    def inproj(rnd):
        ncol = 768 if rnd == 0 else 770
        nv = ncol - 512
        bar = list(st["phase_bar"])

        def emit_squares(j, xe):
            s_ = j % 2
            se = []
            for hf in range(2):
                se.append(sc.op("act", lambda e, hf=hf, s_=s_: e.activation(
                    sq[:, 8 * hf:8 * hf + 8, :], xb[:, s_, 8 * hf:8 * hf + 8, :], AF.Square),
                    [xe[hf]] + st["sq_free"] + bar))
            return se
        wev = st["wev"][rnd]
        pe_last_tile = None
        for j in range(8):
            s_ = j % 2
            t0 = 512 * j
            if j in st["xe"][rnd]:
                xe = st["xe"][rnd][j]
            else:
                xe = x_tile_dma(j, [])
            readers = []
            if rnd == 0:
                if j == 0:
                    se = emit_squares(0, xe)
                else:
                    se = st["se_next"]
                readers += se
                bk = PS[0]
                for c in range(NCH):
                    ev = sc.op("pe", lambda e, c=c, bk=bk: e.matmul(
                        bk[:, :], ones_bf, sq[:, c, :], start=(c == 0), stop=(c == NCH - 1)),
                        [se[c // 8], st["bank_free"][0]] + ev_c + bar, signal=(c == NCH - 1))
                st["sq_free"] = [ev]
                e1 = sc.op("act", lambda e, bk=bk, t0=t0: e.activation(
                    rstd_bc[:, t0:t0 + 512], bk[:, :], AF.Ln, bias=EPS, scale=1.0 / D), [ev])
                e2 = sc.op("act", lambda e, t0=t0: e.activation(
                    rstd_bc[:, t0:t0 + 512], rstd_bc[:, t0:t0 + 512], AF.Exp, scale=-0.5), [e1])
                st["bank_free"][0] = e1
                st["rstd_ev"][j] = e2
            rs_ev = st["rstd_ev"][j]
            for eb in range(4):
                bi = 2 + (eb % 2)
                bk = PS[bi]
                for c in range(NCH):
                    ev = sc.op("pe", lambda e, c=c, bk=bk, eb=eb, s_=s_: e.matmul(
                        bk[:, :], wbf[:, c, 128 * eb:128 * eb + 128], xb[:, s_, c, :],
                        start=(c == 0), stop=(c == NCH - 1)),
                        [xe[c // 8], wev[c][0], wev[c][1], st["bank_free"][bi]] + bar, signal=(c == NCH - 1))
                dst = qTs[rnd] if eb < 2 else kTs[rnd]
                e1 = sc.op("dve", lambda e, bk=bk, dst=dst, eb=eb, t0=t0: e.tensor_tensor(
                    dst[:, eb % 2, t0:t0 + 512], bk[:, :], rstd_bc[:, t0:t0 + 512], ALU.mult), [ev, rs_ev] + bar)
                st["bank_free"][bi] = e1
            if rnd == 0 and j + 1 < 8:
                if (j + 1) not in st["xe"][rnd]:
                    st["xe"][rnd][j + 1] = x_tile_dma(j + 1, [])
                st["se_next"] = emit_squares(j + 1, st["xe"][rnd][j + 1])
            if rnd == 0:
                bk1 = PS[1]
                for i in range(4):
                    ev = sc.op("pe", lambda e, i=i, bk1=bk1, t0=t0: e.matmul(
                        bk1[:, i:i + 1], rstd_bc[:, t0 + 128 * i:t0 + 128 * i + 128], e0, start=True, stop=True),
                        [rs_ev, st["bank_free"][1]], signal=(i == 3))
                e3 = sc.op("dve", lambda e, bk1=bk1, j=j: e.tensor_copy(
                    out=rstd_col[:, 4 * j:4 * j + 4], in_=bk1[:, 0:4]), [ev])
                st["bank_free"][1] = e3
                st["rcol_ev"][j] = e3
            rc_ev = st["rcol_ev"][j]
            for i in range(4):
                bi = 4 + (i % 2)
                bk = PS[bi]
                blk = 4 * j + i
                for c in range(NCH):
                    ev = sc.op("pe", lambda e, c=c, bk=bk, i=i, s_=s_: e.matmul(
                        bk[:, 0:nv], xb[:, s_, c, 128 * i:128 * i + 128], wbf[:, c, 512:ncol],
                        start=(c == 0), stop=(c == NCH - 1)),
                        [xe[c // 8], wev[c][1], st["bank_free"][bi]] + bar, signal=(c == NCH - 1))
                pe_last_tile = ev
                e1 = sc.op("act", lambda e, bk=bk, blk=blk: e.activation(
                    Vts[rnd][:, blk, :], bk[:, 0:256], AF.Copy, scale=rstd_col[:, blk:blk + 1]), [ev, rc_ev] + bar)
                fr = [e1]
                if rnd == 1:
                    e2 = sc.op("dve", lambda e, bk=bk, blk=blk: e.scalar_tensor_tensor(
                        fz[:, 2 * blk:2 * blk + 2], bk[:, 256:258], rstd_col[:, blk:blk + 1], bf_bc, ALU.mult, ALU.add),
                        [ev, rc_ev] + ev_c + bar)
                    fr.append(e2)
                st["bank_free"][bi] = fr
            st["xb_free"][s_] = [pe_last_tile] + readers
        st["pe_last_inproj"] = pe_last_tile
        st["phase_bar"] = sc.all_last()

    st["rstd_ev"] = [None] * 8
    st["rcol_ev"] = [None] * 8

    def fox_c_stages():
        F = {}

        def dep0():
            return [st["fz_last"], st["pe_last_inproj"]]

        def s_a():
            F["e1"] = sc.op("act", lambda e: e.activation(spt[:, :], fz[:, :], AF.Exp, scale=-1.0), dep0())
            F["e2"] = sc.op("act", lambda e: e.activation(spt[:, :], spt[:, :], AF.Ln, bias=1.0), [F["e1"]])
            F["ev"] = sc.op("dve", lambda e: e.memset(acc[:, 0:2], 0.0), dep0())

        def s_adds(b0, b1):
            def f():
                ev = F["ev"]
                for b in range(b0, b1):
                    ev = sc.op("dve", lambda e, b=b: e.tensor_tensor(
                        acc[:, 2 * b + 2:2 * b + 4], acc[:, 2 * b:2 * b + 2], spt[:, 2 * b:2 * b + 2], ALU.add),
                        [ev, F["e2"]])
                F["ev"] = ev
            return f

        def s_mm():
            bk = PS[3]
            sc.op("pe", lambda e: e.matmul(bk[:, 0:64], triincl_f, spt[:, :], start=True, stop=False),
                  [F["e2"]] + ev_c + dep0(), signal=False)
            F["pm"] = sc.op("pe", lambda e: e.matmul(bk[:, 0:64], ones_f, acc[:, :], start=False, stop=True),
                            [F["ev"]])

        def s_split():
            bk = PS[3]
            cn = sc.op("dve", lambda e: e.tensor_copy(out=cneg[:, :], in_=bk[:, 0:64]), [F["pm"]])
            F["cn"] = cn
            cflat = cneg[:, :]
            a = sc.op("dve", lambda e: e.tensor_copy(out=hib[:, :], in_=cflat), [cn])
            a = sc.op("dve", lambda e: e.tensor_copy(out=hif[:, :], in_=hib[:, :]), [a])
            a = sc.op("dve", lambda e: e.tensor_tensor(r1[:, :], cflat, hif[:, :], ALU.subtract), [a])
            a = sc.op("dve", lambda e: e.tensor_copy(out=midb[:, :], in_=r1[:, :]), [a])
            a = sc.op("dve", lambda e: e.tensor_copy(out=midf[:, :], in_=midb[:, :]), [a])
            a = sc.op("dve", lambda e: e.tensor_tensor(r2[:, :], r1[:, :], midf[:, :], ALU.subtract), [a])
            a = sc.op("dve", lambda e: e.tensor_copy(out=lob[:, :], in_=r2[:, :]), [a])
            F["a"] = a

        def s_xh():
            a = F["a"]
            F["zr"] = sc.op("dve", lambda e: e.memset(RT3[:, :, :], 0.0), dep0())
            for h in range(2):
                for k, src in enumerate((hib, midb, lob)):
                    srcv = src[:, :].rearrange("p (b h) -> p b h", h=2)
                    a = sc.op("dve", lambda e, h=h, k=k, srcv=srcv: e.tensor_scalar(
                        Xh[:, h, 32 * k:32 * k + 32], srcv[:, :, h], -1.0, None, ALU.mult), [a])
            F["a"] = a

        def s_tr():
            F["c"] = []
            for h in range(2):
                bkh = PS[6] if h == 0 else PS[3]
                t = sc.op("pe", lambda e, h=h, bkh=bkh: e.matmul(
                    bkh[0:96, 0:128], Xh[:, h, :], ident_f, start=True, stop=True), [F["a"], F["cn"]] + ev_c)
                F["c"].append(sc.op("act", lambda e, h=h, bkh=bkh: e.activation(
                    trs[0:96, h, :], bkh[0:96, 0:128], AF.Copy), [t]))

        def s_dma():
            evs = []
            for h in range(2):
                d1 = sc.dma("sp", ld_c, lambda e, h=h: e.dma_start(out=dscr[h].ap(), in_=trs[0:96, h, :]),
                            [F["c"][h]])
                d2 = sc.dma("sp", ld_c, lambda e, h=h: e.dma_start(
                    out=RT3[0:3, h, :], in_=dscr[h].ap().rearrange("(k b) t -> k (b t)", k=3)), [d1, F["zr"]])
                evs.append(d2)
            st["rt3_ev"] = evs[-1]
            st["cneg_ev"] = F["cn"]
        return [s_a, s_adds(0, 8), s_adds(8, 16), s_adds(16, 24), s_adds(24, 31), s_mm, s_split, s_xh, s_tr, s_dma]

    stg_sem = [sc.dsem("stg0"), sc.dsem("stg1")]
    cc_sem = sc.new_sem("cc")
    st["cc_n"] = 0
    stg_free = [None, None]

    def attention(rnd, bg=(), filler=(), filler_start=0, late=()):
        bar = list(st["phase_bar"])
        nz = 3
        zb = [PS[i] for i in range(nz)] if rnd == 0 else [PS[5], PS[6], PS[7]]
        ob = [PS[4], PS[5]] if rnd == 0 else [PS[0], PS[1]]
        filler = list(filler)
        fpos = [0]
        late = list(late)
        late_pos = [0]
        late_steps = [0, 1, 2, 3, 4, 6, 8, 10, 13, 17]
        lb = [None, None] if rnd == 0 else [PS[2], PS[3]]
        sb_ = PS[7] if rnd == 0 else PS[4]
        tiles = []
        for h in range(2):
            for G in range(8):
                nkb = 4 * G + 4
                for n in range(nkb):
                    tiles.append((h, G, n, nkb - 1 - n, nkb))
        NT = len(tiles)
        z_free = [None] * nz
        e_ln = [None] * NT
        e_pool = [None] * NT
        e_pe1 = [None] * NT
        e_pe2 = [None] * NT
        e_pe3 = [None] * NT
        e_a1 = [None] * NT
        e_a3 = [None] * NT
        o_free = [None, None]
        l_free = [None, None]
        stat_free = [None]
        deferred = {}
        epi_state = {"i": 0}

        def defer(step, fn):
            deferred.setdefault(step, []).append(fn)

        for step_, fn_ in bg:
            defer(step_, fn_)

        def pe1(i):
            h, G, n, kb, nkb = tiles[i]
            sl = i % nz
            Z = zb[sl]
            diag = kb >= 4 * G
            w = [z_free[sl]] + bar + [ev_c]
            if rnd == 0:
                ev = sc.op("pe", lambda e: e.matmul(
                    Z[:, :], kTs[rnd][:, h, 128 * kb:128 * kb + 128], qTs[rnd][:, h, 512 * G:512 * G + 512],
                    start=True, stop=(not diag)), w, signal=(not diag))
                if diag:
                    ev = sc.op("pe", lambda e: e.matmul(
                        Z[:, :], ident_bf, nm_sb(kb - 4 * G), start=False, stop=True), [st_m["ev"]])
            else:
                sc.op("pe", lambda e: e.matmul(
                    Z[:, :], kTs[rnd][:, h, 128 * kb:128 * kb + 128], qTs[rnd][:, h, 512 * G:512 * G + 512],
                    start=True, stop=False), w, signal=False)
                ev = sc.op("pe", lambda e: e.matmul(
                    Z[:, :], ones_bf, RT3[:, h, 512 * G:512 * G + 512],
                    start=False, stop=(not diag)), [st["rt3_ev"]], signal=(not diag))
                if diag:
                    ev = sc.op("pe", lambda e: e.matmul(
                        Z[:, :], ident_bf, nm_fx(kb - 4 * G), start=False, stop=True), [st_m["ev"]])
            e_pe1[i] = ev

        def act12(i):
            h, G, n, kb, nkb = tiles[i]
            sl = i % nz
            Z = zb[sl]
            a1 = sc.op("act", lambda e: e.activation(Eb[:, sl, :], Z[:, :], AF.Exp), [e_pe1[i]] + bar)
            prev = i - nz
            w = [a1]
            if prev >= 0:
                w += [e_pe2[prev], e_pool[prev]]
            a2 = sc.op("act", lambda e: e.activation(Lb[:, sl, :], Eb[:, sl, :], AF.Ln, bias=1.0), w)
            e_ln[i] = a2
            if n < nkb - 1:
                w = [a2] + bar
                if i >= 1:
                    w.append(e_pe2[i - 1])
                if n == 0:
                    e_pool[i] = sc.op("dve", lambda e: e.tensor_copy(out=LS[:, 1, :], in_=Lb[:, sl, :]), w)
                else:
                    e_pool[i] = sc.op("dve", lambda e: e.tensor_tensor(
                        LS[:, (n + 1) % 2, :], LS[:, n % 2, :], Lb[:, sl, :], ALU.add), w + [e_pool[i - 1]])

        def pe2(i):
            h, G, n, kb, nkb = tiles[i]
            sl = i % nz
            Z = zb[sl]
            ev = sc.op("pe", lambda e: e.matmul(Z[:, :], trineg, Lb[:, sl, :], start=False, stop=(n == 0),
                                                skip_group_check=True),
                       [e_ln[i]], signal=(n == 0))
            if n > 0:
                ev = sc.op("pe", lambda e: e.matmul(Z[:, :], onesneg, LS[:, n % 2, :], start=False, stop=True,
                                                    skip_group_check=True),
                           [e_pool[i - 1]])
            e_pe2[i] = ev

        def act3(i):
            sl = i % nz
            Z = zb[sl]
            prev = i - nz
            w = [e_pe2[i]]
            if prev >= 0:
                w.append(e_pe3[prev])
            e_a3[i] = sc.op("act", lambda e: e.activation(Ab[:, sl, :], Z[:, :], AF.Exp), w)
            z_free[sl] = e_a3[i]

        def act_fx(i):
            h, G, n, kb, nkb = tiles[i]
            sl = i % nz
            Z = zb[sl]
            prev = i - nz
            w = [e_pe1[i], st["cneg_ev"]] + bar
            if prev >= 0:
                w.append(e_pe3[prev])
            e_a3[i] = sc.op("act", lambda e: e.activation(
                Ab[:, sl, :], Z[:, :], AF.Exp, bias=cneg[:, 2 * kb + h:2 * kb + h + 1]), w)
            z_free[sl] = e_a3[i]

        def pe3(i, step):
            h, G, n, kb, nkb = tiles[i]
            sl = i % nz
            O = ob[G % 2]
            w = [e_a3[i]]
            if n == 0:
                w.append(o_free[G % 2])
            last = (n == nkb - 1)
            ev = sc.op("pe", lambda e: e.matmul(
                O[:, :], Vts[rnd][:, kb, 128 * h:128 * h + 128], Ab[:, sl, :], start=(n == 0), stop=last), w,
                signal=(rnd == 0 or last))
            ev_o = ev
            if rnd == 1:
                Lk = lb[G % 2]
                w2 = [l_free[G % 2]] if n == 0 else []
                ev = sc.op("pe", lambda e: e.matmul(
                    Lk[:, :], ones_bf, Ab[:, sl, :], start=(n == 0), stop=last), w2)
            e_pe3[i] = ev
            if last:
                epilogue(h, G, ev_o, step, ev)

        def epilogue(h, G, ev_o, step, ev_acc=None):
            hl = 2 * rnd + h
            O = ob[G % 2]
            k = epi_state["i"] % 2
            epi_state["i"] += 1
            src = O[:, :]
            src_ev = [ev_o]
            a1 = sc.op("act", lambda e: e.activation(sqo[:, k, :], O[:, :], AF.Square), [ev_o])
            a0 = None
            if rnd == 1:
                a0 = ev_acc

            Pp = {"a0": a0}

            def part1():
                Pp["pm"] = sc.op("pe", lambda e: e.matmul(sb_[:, :], ones_bf, sqo[:, k, :], start=True, stop=True),
                                 [a1, stat_free[0]])
                if rnd == 1:
                    Lk_ = lb[G % 2]
                    Pp["a0"] = sc.op("act", lambda e: e.activation(
                        rl[:, k, :], Lk_[:, :], AF.Square, scale=EPS ** 0.5), [Pp["a0"]])
                    l_free[G % 2] = Pp["a0"]

            def part2():
                if rnd == 0:
                    d3 = sc.op("act", lambda e: e.activation(
                        rr[:, k, :], sb_[:, :], AF.Ln, bias=EPS, scale=1.0 / 128), [Pp["pm"]])
                    stat_free[0] = d3
                else:
                    d3a = sc.op("dve", lambda e: e.scalar_tensor_tensor(
                        rr[:, k, :], sb_[:, :], 1.0 / 128, rl[:, k, :], ALU.mult, ALU.add), [Pp["pm"], Pp["a0"]])
                    stat_free[0] = d3a
                    d3 = sc.op("act", lambda e: e.activation(rr[:, k, :], rr[:, k, :], AF.Ln), [d3a])
                Pp["d3"] = d3

            def part3():
                d4 = sc.op("act", lambda e: e.activation(rr[:, k, :], rr[:, k, :], AF.Exp, scale=-0.5), [Pp["d3"]])
                d5 = sc.op("dve", lambda e: e.scalar_tensor_tensor(
                    stg[:, k, :], src, gout[:, hl:hl + 1], rr[:, k, :], ALU.mult, ALU.mult),
                    [d4, stg_free[k]] + src_ev)
                o_free[G % 2] = d5
                if hl < 3:
                    dst_ap = cin[hl].ap()[:, 512 * G:512 * G + 512]
                elif G < 6:
                    dst_ap = cinA.ap()[:, 512 * G:512 * G + 512]
                else:
                    dst_ap = cinB.ap()[:, 512 * (G - 6):512 * (G - 6) + 512]
                dm = sc.dma("sp", stg_sem[k], lambda e: e.dma_start(out=dst_ap, in_=stg[:, k, :]), [d5])
                stg_free[k] = dm
                if G == 7 or (hl == 3 and G == 5):
                    if hl < 3:
                        gi, go = cin[hl], cout[hl]
                    elif G == 5:
                        gi, go = cinA, coutA
                    else:
                        gi, go = cinB, coutB

                    def gather(dm=dm, gi=gi, go=go, dmprev=[stg_free[1 - k]]):
                        st["cc_n"] += 1
                        n_ = st["cc_n"]
                        ws = sc._waits("pool", [dm, dmprev[0]])
                        sc.ops["pool"].append((lambda e: e.collective_compute(
                            "AllGather", ALU.bypass, replica_groups=[[0, 1, 2, 3], [4, 5, 6, 7]],
                            ins=[gi.ap().opt()], outs=[go.ap().opt()]), ws, (cc_sem, n_), 1))
                    if hl == 3 and G == 7:
                        gather()
                    else:
                        defer(step + 9, gather)
            defer(step + 3, part1)
            defer(step + 4, part2)
            defer(step + 5, part3)

        nsteps = NT + 16
        pe1(0)
        for s_ in range(nsteps):
            for fn in deferred.pop(s_, []):
                fn()
            if rnd == 0:
                if 0 <= s_ - 1 < NT:
                    pe2(s_ - 1)
                if 0 <= s_ - 2 < NT:
                    pe3(s_ - 2, s_)
                if s_ + 1 < NT:
                    pe1(s_ + 1)
                if s_ < NT:
                    act12(s_)
                if 0 <= s_ - 1 < NT:
                    act3(s_ - 1)
                if filler and s_ >= filler_start:
                    span = max(1, NT - 38 - filler_start)
                    target = min(len(filler), ((s_ - filler_start + 1) * len(filler) + span - 1) // span)
                    while fpos[0] < target:
                        filler[fpos[0]]()
                        fpos[0] += 1
                if late and fpos[0] >= len(filler) and s_ >= NT - 34:
                    li = late_pos[0]
                    if li < len(late) and (s_ - (NT - 34)) >= late_steps[li]:
                        late[li]()
                        late_pos[0] += 1
            else:
                if 0 <= s_ - 1 < NT:
                    pe3(s_ - 1, s_)
                if s_ + 1 < NT:
                    pe1(s_ + 1)
                if s_ < NT:
                    act_fx(s_)
        for k_ in sorted(deferred):
            for fn in deferred[k_]:
                fn()
        while fpos[0] < len(filler):
            filler[fpos[0]]()
            fpos[0] += 1
        while late_pos[0] < len(late):
            late[late_pos[0]]()
            late_pos[0] += 1
        st["phase_bar"] = sc.all_last() + ([st["rt3_ev"]] if late else [])

    xo_sem = [sc.dsem("xo%d" % i) for i in range(4)]
    mx_sem = sc.dsem("mx")
    mx2_sem = sc.dsem("mx2")
    mxa_sem = sc.dsem("mxa")
    wo2_sem = sc.dsem("wo2")
    wo2a_sem = sc.dsem("wo2a")
    wo_sem = [sc.dsem("wo0"), sc.dsem("wo1")]

    def outproj():
        bar = list(st["phase_bar"]) + [stg_free[0], stg_free[1]]
        ccw = (cc_sem, 4)
        ccw5 = (cc_sem, 5)
        me = []

        def cv(t):
            return t.ap().rearrange("(r p) t -> p r t", p=128)
        cc3 = (cc_sem, 3)

        def mload(eng, sem, hl, part, src, wait):
            if part == 0:
                return sc.dma(eng, sem, lambda e: e.dma_start(
                    out=mixT[:, 4 * hl:4 * hl + 4, 0:768],
                    in_=cv(src)[:, :, bass.ds(gidx(eng) * 768, 768)]), [wait] + bar)
            off_ = 3072 if src.shape[1] == S else 0
            return sc.dma(eng, sem, lambda e: e.dma_start(
                out=mixT[:, 4 * hl:4 * hl + 4, 768:1024],
                in_=cv(src)[:, :, bass.ds(gidx(eng) * 256 + off_, 256)]), [wait] + bar)
        m_sp = [mload("sp", mx_sem, 0, 0, cout[0], cc3), mload("sp", mx_sem, 0, 1, cout[0], cc3),
                mload("sp", mx_sem, 2, 1, cout[2], cc3)]
        m_act = [mload("act", mxa_sem, 1, 0, cout[1], cc3), mload("act", mxa_sem, 1, 1, cout[1], cc3),
                 mload("act", mxa_sem, 2, 0, cout[2], cc3)]
        me = [m_sp[-1], m_act[-1]]
        xown_v = xown.rearrange("(c p) t -> p c t", p=128)
        xe = []
        for q4 in range(4):
            xe.append(sc.dma("sp", xo_sem[q4], lambda e, q4=q4: e.dma_start(
                out=x1T[:, 4 * q4:4 * q4 + 4, :], in_=xown_v[:, 4 * q4:4 * q4 + 4, :]), bar))
        me2 = [mload("sp", mx2_sem, 3, 0, coutA, ccw), mload("sp", mx2_sem, 3, 1, coutB, ccw5)]
        wout_v = wout.rearrange("(e p) d -> p e d", p=128)
        wo_free = [[], []]
        bank_free = [None] * 8
        sq_free = [None] * 4
        pend = []
        cnt = 0
        ss_ev = [None, None]
        we2a = sc.dma("pool", wo2a_sem, lambda e: e.dma_start(out=wo2a[:, :, :], in_=wout_v[:, 12:14, :]), bar)
        for dg in range(4):
            s_ = dg % 2
            if dg in st["wo_pre"]:
                we = st["wo_pre"][dg]
            else:
                we = sc.dma("pool", wo_sem[s_], lambda e, dg=dg, s_=s_: e.dma_start(
                    out=wo[:, s_, :, :], in_=wout_v[:, 0:12, 512 * dg:512 * dg + 512]), wo_free[s_] + bar)
            lastpe = None
            for dbl in range(4):
                db = 4 * dg + dbl
                for th in range(2):
                    bi = cnt % 3
                    cnt += 1
                    bk = PS[bi]
                    for e_ in range(12):
                        ev = sc.op("pe", lambda e, e_=e_, bk=bk, s_=s_, dbl=dbl, th=th: e.matmul(
                            bk[:, :], wo[:, s_, e_, 128 * dbl:128 * dbl + 128], mixT[:, e_, 512 * th:512 * th + 512],
                            start=(e_ == 0), stop=(e_ == 11)),
                            [we, me, bank_free[bi]] + bar, signal=(e_ == 11))
                    lastpe = ev
                    d1 = sc.op("dve", lambda e, bk=bk, db=db, th=th: e.tensor_tensor(
                        x1T[:, db, 512 * th:512 * th + 512], bk[:, :], x1T[:, db, 512 * th:512 * th + 512], ALU.add),
                        [ev, xe[db // 4]] + bar)
                    bank_free[bi] = d1
            wo_free[s_] = [lastpe]
        xn_ev = {}
        if st["wo2_pre"] is not None:
            we2b = st["wo2_pre"]
        else:
            we2b = sc.dma("pool", wo2_sem, lambda e: e.dma_start(out=wo2b[:, :, :], in_=wout_v[:, 14:16, :]), bar)
        we2 = [we2a, we2b]
        for db in range(16):
            for th in range(2):
                bi = cnt % 3
                bk = PS[bi]
                for e_ in range(4):
                    ev = sc.op("pe", lambda e, e_=e_, bk=bk, db=db, th=th: e.matmul(
                        bk[:, :], (wo2a if e_ < 2 else wo2b)[:, e_ % 2, 128 * db:128 * db + 128],
                        mixT[:, 12 + e_, 512 * th:512 * th + 512],
                        start=(e_ == 0), stop=(e_ == 3)),
                        [we2, me2[-1], bank_free[bi]] + bar, signal=(e_ == 3))
                d1 = sc.op("dve", lambda e, bk=bk, db=db, th=th: e.tensor_tensor(
                    x1T[:, db, 512 * th:512 * th + 512], bk[:, :], x1T[:, db, 512 * th:512 * th + 512], ALU.add),
                    [ev])
                bank_free[bi] = d1
                k = cnt % 4
                a1 = sc.op("act", lambda e, db=db, th=th, k=k: e.activation(
                    sq2[:, k, :], x1T[:, db, 512 * th:512 * th + 512], AF.Square), [d1, sq_free[k]])
                xn_ev[(db, th)] = sc.op("act", lambda e, db=db, th=th: e.activation(
                    x1n[:, db, 512 * th:512 * th + 512], x1T[:, db, 512 * th:512 * th + 512], AF.Copy,
                    scale=gmlp[:, db:db + 1]), [d1])
                for fn in pend:
                    fn()
                pend = []

                def stat(a1=a1, k=k, th=th, db=db):
                    ev2 = sc.op("pe", lambda e: e.matmul(
                        PS[6 + th][:, :], ones_bf, sq2[:, k, :], start=(db == 0), stop=(db == 15)), [a1] + ev_c)
                    sq_free[k] = ev2
                    ss_ev[th] = ev2
                pend.append(stat)
                cnt += 1
        for fn in pend:
            fn()
        nx = []
        for th in range(2):
            d1 = sc.op("act", lambda e, th=th: e.activation(
                r2bc[:, 512 * th:512 * th + 512], PS[6 + th][:, :], AF.Ln, bias=EPS, scale=1.0 / D), [ss_ev[th]])
            d2 = sc.op("act", lambda e, th=th: e.activation(
                r2bc[:, 512 * th:512 * th + 512], r2bc[:, 512 * th:512 * th + 512], AF.Exp, scale=-1.0), [d1])
            nx.append(d2)
        st["r2_ev"] = nx
        st["xn_ev"] = xn_ev
        st["phase_bar"] = sc.all_last()

    wu_sem = [sc.dsem("wu0"), sc.dsem("wu1")]
    wd_sem = [sc.dsem("wd%d" % i) for i in range(3)]

    def ffn():
        bar = list(st["phase_bar"])
        xn_ev = st["xn_ev"]
        wup_v = wup.rearrange("(c p) f -> p c f", p=128)
        wdn_v = wdn.rearrange("(fb p) d -> p fb d", p=128)
        wu_free = [[], []]
        wd_free = [[], [], []]
        ubank = [PS[0], PS[1], PS[2]]
        dbank = [PS[3], PS[4], PS[5]]
        u_free = [None] * 3
        tmp_free = [None, None]
        d_free = [None] * 3
        rT_w = {}
        rT_free = []
        ucnt = 0
        dcnt = 0
        wucnt = 0
        wdcnt = 0
        last_evac = None
        fcnt = [0]
        fsq_free = [None] * 4
        fpend = []
        st["fin_ss"] = [None, None]
        for fg in range(8):
            new_rT_free = []
            for sg in range(2):
                s_ = wucnt % 2
                wucnt += 1
                f0 = 1024 * fg + 512 * sg
                if wucnt == 1 and st["wu_pre"] is not None:
                    we = st["wu_pre"]
                else:
                    we = sc.dma("pool", wu_sem[s_], lambda e, s_=s_, f0=f0: e.dma_start(
                        out=wu_t[s_][:, :, :], in_=wup_v[:, :, f0:f0 + 512]), wu_free[s_] + bar)
                lastpe = None
                for fbl in range(4):
                    fb = 4 * sg + fbl
                    for th in range(2):
                        bi = ucnt % 3
                        ucnt += 1
                        bk = ubank[bi]
                        for c in range(NCH):
                            ev = sc.op("pe", lambda e, c=c, bk=bk, s_=s_, fbl=fbl, th=th: e.matmul(
                                bk[:, :], wu_t[s_][:, c, 128 * fbl:128 * fbl + 128], x1n[:, c, 512 * th:512 * th + 512],
                                start=(c == 0), stop=(c == NCH - 1)),
                                [we, xn_ev[(c, th)], u_free[bi]] + bar, signal=(c == NCH - 1))
                        lastpe = ev
                        k = (ucnt - 1) % 2
                        a1 = sc.op("act", lambda e, bk=bk, k=k: e.activation(rtmp[:, k, :], bk[:, :], AF.Relu),
                                   [ev, tmp_free[k]] + bar)
                        u_free[bi] = a1
                        a2 = sc.op("act", lambda e, k=k: e.activation(rtmp[:, k, :], rtmp[:, k, :], AF.Square), [a1])
                        d1 = sc.op("dve", lambda e, k=k, fb=fb, th=th: e.tensor_tensor(
                            rT[:, fb, 512 * th:512 * th + 512], rtmp[:, k, :], r2bc[:, 512 * th:512 * th + 512],
                            ALU.mult), [a2, st["r2_ev"][th]] + rT_free + bar)
                        tmp_free[k] = d1
                        rT_w[(fb, th)] = d1
                wu_free[s_] = [lastpe]
            for dg in range(4):
                s_ = wdcnt % 3
                wdcnt += 1
                we = sc.dma("pool", wd_sem[s_], lambda e, s_=s_, fg=fg, dg=dg: e.dma_start(
                    out=wd[:, s_, :, :], in_=wdn_v[:, 8 * fg:8 * fg + 8, 512 * dg:512 * dg + 512]),
                    wd_free[s_] + bar)
                lastpe = None
                for dbl in range(4):
                    db = 4 * dg + dbl
                    for th in range(2):
                        bi = dcnt % 3
                        dcnt += 1
                        bk = dbank[bi]
                        for fb in range(8):
                            ev = sc.op("pe", lambda e, fb=fb, bk=bk, s_=s_, dbl=dbl, th=th: e.matmul(
                                bk[:, :], wd[:, s_, fb, 128 * dbl:128 * dbl + 128], rT[:, fb, 512 * th:512 * th + 512],
                                start=(fb == 0), stop=(fb == 7)),
                                [we, rT_w[(fb, th)], d_free[bi]], signal=(fb == 7))
                        lastpe = ev
                        d1 = sc.op("dve", lambda e, bk=bk, db=db, th=th: e.tensor_tensor(
                            x1T[:, db, 512 * th:512 * th + 512], bk[:, :], x1T[:, db, 512 * th:512 * th + 512],
                            ALU.add), [ev])
                        d_free[bi] = d1
                        last_evac = d1
                        if fg == 7:
                            k = fcnt[0] % 4
                            fcnt[0] += 1
                            a1 = sc.op("act", lambda e, db=db, th=th, k=k: e.activation(
                                sq2[:, k, :], x1T[:, db, 512 * th:512 * th + 512], AF.Square), [d1, fsq_free[k]])

                            def fstat(a1=a1, k=k, th=th, db=db):
                                ev2 = sc.op("pe", lambda e: e.matmul(
                                    PS[6 + th][:, :], ones_bf, sq2[:, k, :], start=(db == 0), stop=(db == 15)),
                                    [a1] + ev_c)
                                fsq_free[k] = ev2
                                st["fin_ss"][th] = ev2
                            fpend.append(fstat)
                            while len(fpend) > 2:
                                fpend.pop(0)()
                wd_free[s_] = [lastpe]
                new_rT_free = [lastpe]
            rT_free = new_rT_free
        for fn in fpend:
            fn()
        st["phase_bar"] = sc.all_last()

    out_sem = sc.dsem("out")

    def final():
        bar = list(st["phase_bar"])
        ss_ev = st["fin_ss"]
        nx = []
        for th in range(2):
            d1 = sc.op("act", lambda e, th=th: e.activation(
                r2bc[:, 512 * th:512 * th + 512], PS[6 + th][:, :], AF.Ln, bias=EPS, scale=1.0 / D),
                [ss_ev[th]] + bar)
            d2 = sc.op("act", lambda e, th=th: e.activation(
                r2bc[:, 512 * th:512 * th + 512], r2bc[:, 512 * th:512 * th + 512], AF.Exp, scale=-0.5), [d1])
            nx.append(d2)
        yT_v = yT.rearrange("(c p) t -> p c t", p=128)
        evs = []
        for c in range(NCH):
            eng = "dve"
            d = sc.op(eng, lambda e, c=c: e.scalar_tensor_tensor(
                x1T[:, c, :], x1T[:, c, :], gfin[:, c:c + 1], r2bc[:, :], ALU.mult, ALU.mult),
                [nx[0], nx[1]] + bar)
            evs.append(d)
            if c % 4 == 3:
                st["out_ev"] = sc.dma("sp", out_sem, lambda e, c=c: e.dma_start(
                    out=yT_v[:, c - 3:c + 1, :], in_=x1T[:, c - 3:c + 1, :]), evs[-4:])

    st["wo_pre"] = {}
    st["wu_pre"] = None
    st["wo2_pre"] = None

    def prefetch_jobs():
        wout_v = wout.rearrange("(e p) d -> p e d", p=128)
        wup_v = wup.rearrange("(c p) f -> p c f", p=128)
        war = list(st["att0_end"])

        def wo_job(dg):
            st["wo_pre"][dg] = sc.dma("pool", wo_sem[dg], lambda e: e.dma_start(
                out=wo[:, dg, :, :], in_=wout_v[:, 0:12, 512 * dg:512 * dg + 512]), war)

        def wu_job():
            st["wu_pre"] = sc.dma("pool", wu_sem[0], lambda e: e.dma_start(
                out=wu0[:, :, :], in_=wup_v[:, :, 0:512]), war)
        def wo2_job():
            st["wo2_pre"] = sc.dma("pool", wo2_sem, lambda e: e.dma_start(
                out=wo2b[:, :, :], in_=wout_v[:, 14:16, :]), war)
        return [(10, lambda: wo_job(0)), (70, lambda: wo_job(1)), (130, wo2_job), (170, wu_job)]

    for kind, fn in inproj_prep(0):
        fn()
    inproj(0)
    st["bar0"] = list(st["phase_bar"])
    if stage >= 2:
        bg = []
        items = []
        if stage >= 3:
            jobs = inproj_prep(1)
            step = 1
            for kind, fn in jobs:
                if kind == "s" and step < 12:
                    step = 12
                bg.append((step, fn))
                step += {"w": 1, "x": 2, "s": 1}[kind]
            items = inproj1_items()
        attention(0, bg, items, 30, fox_c_stages() if stage >= 3 else ())
        st["att0_end"] = list(st["phase_bar"])
    if stage >= 4:
        attention(1, prefetch_jobs() if stage >= 5 else ())
    if stage >= 5:
        outproj()
    if stage >= 6:
        ffn()
        final()

    dump_sem = sc.dsem("dump")
    dump_ev = None
    name2t = {"qT": (qTs[1], [128, 2, S], BF16), "kT": (kTs[1], [128, 2, S], BF16), "Vt": (Vts[1], [128, 32, 256], BF16),
              "rstd_bc": (rstd_bc, [128, S], F32), "rstd_col": (rstd_col, [128, 32], F32),
              "cneg": (cneg, [128, 64], F32), "RT3": (RT3, [128, 2, S], BF16),
              "x1T": (x1T, [128, 16, TOK], F32), "mixT": (mixT, [128, 16, TOK], BF16),
              "x1n": (x1n, [128, 16, TOK], BF16), "r2bc": (r2bc, [128, TOK], F32), "fz": (fz, [128, 64], F32)}
    for nm in dumps:
        bar = sc.all_last()
        if nm.startswith("cin"):
            hl = int(nm[3:])
            dt_ = nc.dram_tensor("dbg_" + nm, [128, S], BF16, kind="ExternalOutput").ap()
            dump_ev = sc.dma("sp", dump_sem, lambda e, dt_=dt_, hl=hl: e.dma_start(out=dt_, in_=cin[hl].ap()),
                             bar + [(stg_sem[0]["h"], stg_sem[0]["n"]), (stg_sem[1]["h"], stg_sem[1]["n"])])
            continue
        t, shp, dty = name2t[nm]
        dt_ = nc.dram_tensor("dbg_" + nm, shp, dty, kind="ExternalOutput").ap()
        if len(shp) == 3:
            dump_ev = sc.dma("sp", dump_sem, lambda e, dt_=dt_, t=t: e.dma_start(out=dt_, in_=t[:, :, :]), bar)
        else:
            dump_ev = sc.dma("sp", dump_sem, lambda e, dt_=dt_, t=t: e.dma_start(out=dt_, in_=t[:, :]), bar)

    fin = []
    if stage >= 6:
        fin.append(st["out_ev"])
    if dump_ev is not None:
        fin.append(dump_ev)
    if st["cc_n"] > 0:
        fin.append((cc_sem, st["cc_n"]))
    fin += sc.all_last()
    for d_ in [ld_c, ld_cb, ld_cm, mx_sem, mx2_sem, mxa_sem, wo2_sem, wo2a_sem] + xo_sem + x_sem[0] + x_sem[1] + x2_sem + w_sem + wo_sem + wu_sem + wd_sem + stg_sem:
        if d_["n"] > 0:
            fin.append((d_["h"], d_["n"]))
    ws = sc._waits("sp", fin)
    sc.ops["sp"].append((lambda e: e.engine_nop() if hasattr(e, "engine_nop") else None, ws, None, 0))

    with nc.Block() as block:
        @block.tensor
        def _(e):
            sc.replay("pe", e)

        @block.scalar
        def _(e):
            if stage >= 5:
                gidx("act")
            sc.replay("act", e)

        @block.vector
        def _(e):
            sc.replay("dve", e)

        @block.gpsimd
        def _(e):
            sc.replay("pool", e)

        @block.sync
        def _(e):
            if stage >= 5:
                gidx("sp")
            for fn, ws_, ev, inc in sc.ops["sp"]:
                for sem, val in ws_:
                    e.wait_ge(sem, val)
                if inc == 0:
                    continue
                ins = fn(e)
                if ev is not None:
                    ins.then_inc(ev[0], inc)
    return nc


def _own_tokens(g):
    return np.concatenate([np.arange(768 * g, 768 * g + 768), np.arange(3072 + 256 * g, 3072 + 256 * g + 256)])


def _consts():
    p = np.arange(128)
    ident = np.eye(128, dtype=np.float32)
    ones = np.ones((128, 128), np.float32)
    trineg = -(p[:, None] >= p[None, :]).astype(np.float32)
    c = np.arange(512)
    nm_sb = np.stack([np.where(128 * j + p[:, None] >= c[None, :], NEG, 0.0) for j in range(4)], 1)
    nm_fx = np.stack([np.where(128 * j + p[:, None] > c[None, :], NEG, 0.0) for j in range(4)], 1)
    cbf = np.concatenate([ident, ones, trineg, -ones, nm_sb.reshape(128, -1), nm_fx.reshape(128, -1)], 1)
    triincl = (p[:, None] <= p[None, :]).astype(np.float32)
    e0 = np.zeros((128, 1), np.float32)
    e0[0, 0] = 1.0
    return cbf.astype(np.float32), triincl, ones, ident, e0


def _prep_inputs(x, g_attn, w_in, b_f, g_out_sb, g_out_fox, w_out, g_mlp, w_up, w_down, g_final):
    x = np.asarray(x, np.float32)
    w_in = np.asarray(w_in, np.float32)[0]
    w_out = np.asarray(w_out, np.float32)[0]
    w_up = np.ascontiguousarray(np.asarray(w_up, np.float32)[0])
    w_down = np.ascontiguousarray(np.asarray(w_down, np.float32)[0])
    g_attn = np.asarray(g_attn, np.float32)[0]
    b_f = np.asarray(b_f, np.float32)[0]
    g_out_sb = np.asarray(g_out_sb, np.float32)[0]
    g_out_fox = np.asarray(g_out_fox, np.float32)[0]
    g_mlp = np.asarray(g_mlp, np.float32)[0]
    g_final = np.asarray(g_final, np.float32)
    cbf, triincl, ones, ident, e0 = _consts()

    def col16(v):
        return np.ascontiguousarray(v.reshape(16, 128).T)

    xTs = [np.ascontiguousarray(x[b].T) for b in range(2)]
    in_maps = []
    for core in range(8):
        b, g = core // 4, core % 4
        h0, h1 = 2 * g, 2 * g + 1

        def hc(base, h):
            return list(range(base + 128 * h, base + 128 * h + 128))
        cols_sb = hc(0, h0) + hc(0, h1) + hc(1024, h0) + hc(1024, h1) + hc(2048, h0) + hc(2048, h1)
        cols_fx = (hc(3072, h0) + hc(3072, h1) + hc(4096, h0) + hc(4096, h1) + hc(5120, h0) + hc(5120, h1)
                   + [6144 + h0, 6144 + h1])
        rows = []
        for hl in range(4):
            for r in range(4):
                if hl < 2:
                    hd = 2 * r + hl
                    rows += list(range(128 * hd, 128 * hd + 128))
                else:
                    hd = 2 * r + hl - 2
                    rows += list(range(1024 + 128 * hd, 1024 + 128 * hd + 128))
        gout = np.stack([g_out_sb[128 * h0:128 * h0 + 128], g_out_sb[128 * h1:128 * h1 + 128],
                         g_out_fox[128 * h0:128 * h0 + 128], g_out_fox[128 * h1:128 * h1 + 128]], 1)
        bfb = np.broadcast_to(b_f[[h0, h1]][None, :], (128, 2))
        cf = np.concatenate([triincl, ones, ident, e0, col16(g_attn), bfb, gout, col16(g_mlp), col16(g_final)],
                            1).astype(np.float32)
        in_maps.append({
            "xT": xTs[b],
            "xown": np.ascontiguousarray(xTs[b][:, _own_tokens(g)]),
            "win_sb": np.ascontiguousarray(w_in[:, cols_sb]),
            "win_fx": np.ascontiguousarray(w_in[:, cols_fx]),
            "wout": np.ascontiguousarray(w_out[rows, :]),
            "wup": w_up,
            "wdn": w_down,
            "cbf": cbf,
            "cf": np.ascontiguousarray(cf),
        })
    return in_maps


_NC_CACHE = {}


def kernel(x, g_attn, w_in, b_f, g_out_sb, g_out_fox, w_out, g_mlp, w_up, w_down, g_final):
    in_maps = _prep_inputs(x, g_attn, w_in, b_f, g_out_sb, g_out_fox, w_out, g_mlp, w_up, w_down, g_final)
    if "nc" not in _NC_CACHE:
        _NC_CACHE["nc"] = _build()
    nc = _NC_CACHE["nc"]
    res = run_bass_kernel_spmd(nc, in_maps, core_ids=list(range(8)))
    y = np.empty((2, S, D), np.float32)
    for core in range(8):
        b, g = core // 4, core % 4
        y[b, _own_tokens(g), :] = res.results[core]["yT"].T
    return y
```
